# Optimizing a Trainium2 kernel written in Bass

```python
import math
import jax, jax.numpy as jnp
from jax import lax
import numpy as np

D_MODEL = 2048
BATCH = 2
SEQ = 4096
DEPTH = 4

HEAD_DIM = 64
Q_BLOCK = 128
RMS_EPS = 1e-6
MASK_VALUE = -1e30
ALIBI_MAX_EXP = 8.0
FOX_HEADS = 7
NSA_HEADS = 8
NSA_KV_GROUPS = 2
NSA_CMP_LEN = 32
NSA_CMP_STRIDE = 16
NSA_SEL_BLOCK = 64
NSA_TOP_N = 16
NSA_WINDOW = 512
NSA_FORCE_SCORE = 1e4
DIL_CONFIGS = ((128, 1), (512, 4), (2048, 16))
DIL_HEADS_PER_GROUP = 3
DIL_HEADS = DIL_HEADS_PER_GROUP * len(DIL_CONFIGS)
DIFF_HEADS = 4
DIFF_QK_DIM = 64
DIFF_V_DIM = 2 * DIFF_QK_DIM
MIX_WIDTH = (FOX_HEADS + NSA_HEADS + DIL_HEADS) * HEAD_DIM + DIFF_HEADS * DIFF_V_DIM
D_FF = ((8 * D_MODEL + 3 * 256 - 1) // (3 * 256)) * 256
NSA_KV = NSA_KV_GROUPS * HEAD_DIM
IN_SPLITS = (
    ('fox_q', FOX_HEADS * HEAD_DIM), ('fox_k', FOX_HEADS * HEAD_DIM), ('fox_v', FOX_HEADS * HEAD_DIM), ('fox_f', FOX_HEADS),
    ('nsa_q', NSA_HEADS * HEAD_DIM),
    ('nsa_cmp_k', NSA_KV), ('nsa_cmp_v', NSA_KV), ('nsa_slc_k', NSA_KV), ('nsa_slc_v', NSA_KV), ('nsa_win_k', NSA_KV), ('nsa_win_v', NSA_KV),
    ('nsa_gate', NSA_HEADS * 3),
    ('dil_q', DIL_HEADS * HEAD_DIM), ('dil_k', DIL_HEADS * HEAD_DIM), ('dil_v', DIL_HEADS * HEAD_DIM),
    ('diff_q', DIFF_HEADS * 2 * DIFF_QK_DIM), ('diff_k', DIFF_HEADS * 2 * DIFF_QK_DIM), ('diff_v', DIFF_HEADS * DIFF_V_DIM),
)
N_IN = sum(w for _, w in IN_SPLITS)

kernel_name = 'hybrid_parallel_head_decoder'


def rms_norm(x, gain):
    xf = x.astype(jnp.float32)
    y = xf * lax.rsqrt(jnp.mean(xf * xf, axis=-1, keepdims=True) + RMS_EPS)
    return (y * gain.astype(jnp.float32)).astype(x.dtype)


def masked_softmax(s, mask):
    p = jax.nn.softmax(jnp.where(mask, s, MASK_VALUE), axis=-1)
    return jnp.where(mask, p, 0.0)


def alibi_slopes(n_heads):
    return jnp.asarray(2.0 ** (-ALIBI_MAX_EXP * np.arange(1, n_heads + 1) / n_heads), jnp.float32)


def blocks_to_seq(o):
    o = jnp.moveaxis(o, 0, 1)
    return o.reshape(o.shape[0], o.shape[1] * o.shape[2], *o.shape[3:])


def fox_attention(q, k, v, f_logit, f_bias):
    B, S, H, dh = q.shape
    scale = dh ** -0.5
    log_f = jax.nn.log_sigmoid(f_logit.astype(jnp.float32) + f_bias.astype(jnp.float32))
    cum = jnp.cumsum(log_f, axis=1).transpose(0, 2, 1)
    pos = jnp.arange(S)

    def block(i):
        qs = i * Q_BLOCK
        tq = qs + jnp.arange(Q_BLOCK)
        qb = lax.dynamic_slice_in_dim(q, qs, Q_BLOCK, 1)
        cq = lax.dynamic_slice_in_dim(cum, qs, Q_BLOCK, 2)
        s = jnp.einsum('bqhd,bkhd->bhqk', qb, k).astype(jnp.float32) * scale
        s = s + cq[..., None] - cum[:, :, None, :]
        p = jax.nn.softmax(jnp.where(pos[None, :] <= tq[:, None], s, MASK_VALUE), axis=-1)
        return jnp.einsum('bhqk,bkhd->bqhd', p.astype(v.dtype), v)

    o = blocks_to_seq(lax.map(block, jnp.arange(S // Q_BLOCK)))
    return o.reshape(B, S, H * dh)


def cmp_to_sel_matrix(n_cmp, n_sel):
    a = NSA_SEL_BLOCK // NSA_CMP_STRIDE
    b = NSA_CMP_LEN // NSA_CMP_STRIDE
    j = np.arange(n_sel)[:, None, None]
    idx = a * j + np.arange(a)[None, :, None] - np.arange(b)[None, None, :]
    jj = np.broadcast_to(j, idx.shape)
    ok = (idx >= 0) & (idx < n_cmp)
    m = np.zeros((n_cmp, n_sel), np.float32)
    np.add.at(m, (idx[ok], jj[ok]), 1.0)
    return m


def nsa_attention(q, k_cmp, v_cmp, k_slc, v_slc, k_win, v_win, gate_logit, cmp_w1, cmp_w2, cmp_pos):
    B, S, H, dh = q.shape
    G = k_cmp.shape[2]
    hpg = H // G
    scale = dh ** -0.5
    slopes = alibi_slopes(H).reshape(G, hpg)[None, :, :, None, None]
    ratio = NSA_CMP_LEN // NSA_CMP_STRIDE
    n_chunk = S // NSA_CMP_STRIDE
    n_cmp = n_chunk - ratio + 1
    n_sel = S // NSA_SEL_BLOCK
    n_top = min(NSA_TOP_N, n_sel)

    def compress(t, w1, w2, pos_emb):
        ch = t.reshape(B, n_chunk, NSA_CMP_STRIDE, G, dh)
        blk = jnp.concatenate([ch[:, r:r + n_cmp] for r in range(ratio)], axis=2)
        blk = (blk + pos_emb[:, None, :]).transpose(0, 1, 3, 2, 4).reshape(B, n_cmp, G, NSA_CMP_LEN * dh)
        return jax.nn.gelu(blk @ w1) @ w2

    kc = compress(k_cmp, cmp_w1[0], cmp_w2[0], cmp_pos[0])
    vc = compress(v_cmp, cmp_w1[1], cmp_w2[1], cmp_pos[1])
    cmp_end = jnp.arange(n_cmp) * NSA_CMP_STRIDE + (NSA_CMP_LEN - 1)
    cmp_to_sel = jnp.asarray(cmp_to_sel_matrix(n_cmp, n_sel), jnp.float32)
    k_blocks = k_slc.transpose(0, 2, 1, 3).reshape(B, G, n_sel, NSA_SEL_BLOCK, dh)
    v_blocks = v_slc.transpose(0, 2, 1, 3).reshape(B, G, n_sel, NSA_SEL_BLOCK, dh)
    pad = ((0, 0), (NSA_WINDOW, 0), (0, 0), (0, 0))
    k_win_p = jnp.pad(k_win, pad)
    v_win_p = jnp.pad(v_win, pad)
    gates = jax.nn.sigmoid(gate_logit).reshape(B, S, G, hpg, 3)
    sel_ids = jnp.arange(n_sel)
    win_offsets = jnp.arange(Q_BLOCK + NSA_WINDOW) - NSA_WINDOW
    sel_offsets = jnp.arange(NSA_SEL_BLOCK)

    def block(i):
        qs = i * Q_BLOCK
        tq = qs + jnp.arange(Q_BLOCK)
        qb = lax.dynamic_slice_in_dim(q, qs, Q_BLOCK, 1).reshape(B, Q_BLOCK, G, hpg, dh)
        dist_c = (tq[:, None] - cmp_end[None, :]).astype(jnp.float32)
        s_c = jnp.einsum('bqgjd,bcgd->bgjqc', qb, kc).astype(jnp.float32) * scale - slopes * dist_c
        p_c = masked_softmax(s_c, dist_c >= 0)
        o_c = jnp.einsum('bgjqc,bcgd->bqgjd', p_c.astype(vc.dtype), vc)
        imp = jnp.einsum('bgjqc,cn->bgqn', p_c, cmp_to_sel)
        cur = (tq // NSA_SEL_BLOCK)[:, None]
        forced = (sel_ids == 0) | (sel_ids == cur) | (sel_ids == cur - 1)
        causal = sel_ids * NSA_SEL_BLOCK <= tq[:, None]
        score = jnp.where(causal, jnp.where(forced, NSA_FORCE_SCORE, imp), MASK_VALUE)
        _, top_idx = lax.top_k(score, n_top)
        flat = top_idx.reshape(B, G, Q_BLOCK * n_top)[..., None, None]
        ks = jnp.take_along_axis(k_blocks, flat, axis=2).reshape(B, G, Q_BLOCK, n_top * NSA_SEL_BLOCK, dh)
        vs = jnp.take_along_axis(v_blocks, flat, axis=2).reshape(B, G, Q_BLOCK, n_top * NSA_SEL_BLOCK, dh)
        kpos = (top_idx[..., None] * NSA_SEL_BLOCK + sel_offsets).reshape(B, G, Q_BLOCK, n_top * NSA_SEL_BLOCK)
        dist_s = (tq[:, None] - kpos).astype(jnp.float32)[:, :, None]
        s_s = jnp.einsum('bqgjd,bgqkd->bgjqk', qb, ks).astype(jnp.float32) * scale - slopes * dist_s
        p_s = masked_softmax(s_s, dist_s >= 0)
        o_s = jnp.einsum('bgjqk,bgqkd->bqgjd', p_s.astype(vs.dtype), vs)
        kw = lax.dynamic_slice_in_dim(k_win_p, qs, Q_BLOCK + NSA_WINDOW, 1)
        vw = lax.dynamic_slice_in_dim(v_win_p, qs, Q_BLOCK + NSA_WINDOW, 1)
        kpos_w = qs + win_offsets
        dist_w = tq[:, None] - kpos_w[None, :]
        mask_w = (dist_w >= 0) & (dist_w < NSA_WINDOW) & (kpos_w[None, :] >= 0)
        s_w = jnp.einsum('bqgjd,bkgd->bgjqk', qb, kw).astype(jnp.float32) * scale - slopes * dist_w.astype(jnp.float32)
        p_w = masked_softmax(s_w, mask_w)
        o_w = jnp.einsum('bgjqk,bkgd->bqgjd', p_w.astype(vw.dtype), vw)
        g = lax.dynamic_slice_in_dim(gates, qs, Q_BLOCK, 1)
        return g[..., 0:1] * o_c + g[..., 1:2] * o_s + g[..., 2:3] * o_w

    o = blocks_to_seq(lax.map(block, jnp.arange(S // Q_BLOCK)))
    return o.reshape(B, S, H * dh)


def dilated_attention(q, k, v):
    B, S, H, dh = q.shape
    scale = dh ** -0.5
    slopes = alibi_slopes(H)
    hpg = DIL_HEADS_PER_GROUP
    k_groups = [k[:, :, g * hpg:(g + 1) * hpg] for g in range(len(DIL_CONFIGS))]
    v_groups = [v[:, :, g * hpg:(g + 1) * hpg] for g in range(len(DIL_CONFIGS))]

    def block(i):
        qs = i * Q_BLOCK
        tq = qs + jnp.arange(Q_BLOCK)
        qb = lax.dynamic_slice_in_dim(q, qs, Q_BLOCK, 1)
        outs, lses = [], []
        for g, (window, dilation) in enumerate(DIL_CONFIGS):
            steps = jnp.arange(window // dilation + 1) * dilation
            kidx = tq[:, None] - steps[None, :]
            valid = kidx >= 0
            kidx = jnp.maximum(kidx, 0)
            kg = k_groups[g][:, kidx]
            vg = v_groups[g][:, kidx]
            s = jnp.einsum('bqhd,bqkhd->bhqk', qb[:, :, g * hpg:(g + 1) * hpg], kg).astype(jnp.float32) * scale
            s = s - slopes[g * hpg:(g + 1) * hpg][:, None, None] * steps.astype(jnp.float32)
            s = jnp.where(valid, s, MASK_VALUE)
            lse = jax.nn.logsumexp(s, axis=-1)
            p = jnp.exp(s - lse[..., None])
            outs.append(jnp.einsum('bhqk,bqkhd->bqhd', p.astype(vg.dtype), vg))
            lses.append(lse)
        w = jax.nn.softmax(jnp.stack(lses, axis=0), axis=0).transpose(0, 1, 3, 2)[..., None]
        return jnp.concatenate([o * w[g].astype(o.dtype) for g, o in enumerate(outs)], axis=2)

    o = blocks_to_seq(lax.map(block, jnp.arange(S // Q_BLOCK)))
    return o.reshape(B, S, H * dh)


def diff_attention(q, k, v, lam_vecs, subln_g, lam_init):
    B, S, H, _, dq = q.shape
    dv = v.shape[-1]
    scale = dq ** -0.5
    lv = lam_vecs.astype(jnp.float32)
    lam = jnp.exp(jnp.sum(lv[0] * lv[1])) - jnp.exp(jnp.sum(lv[2] * lv[3])) + lam_init
    slopes = alibi_slopes(H)[None, :, None, None, None]
    pos = jnp.arange(S)

    def block(i):
        qs = i * Q_BLOCK
        tq = qs + jnp.arange(Q_BLOCK)
        qb = lax.dynamic_slice_in_dim(q, qs, Q_BLOCK, 1)
        dist = tq[:, None] - pos[None, :]
        s = jnp.einsum('bqhmd,bkhmd->bhmqk', qb, k).astype(jnp.float32) * scale - slopes * dist.astype(jnp.float32)
        p = jax.nn.softmax(jnp.where(dist >= 0, s, MASK_VALUE), axis=-1)
        a = p[:, :, 0] - lam * p[:, :, 1]
        return jnp.einsum('bhqk,bkhd->bqhd', a.astype(v.dtype), v)

    o = blocks_to_seq(lax.map(block, jnp.arange(S // Q_BLOCK)))
    o = rms_norm(o, subln_g) * (1.0 - lam_init)
    return o.reshape(B, S, H * dv)


def hybrid_mixer(u, w_in, fox_f_bias, cmp_w1, cmp_w2, cmp_pos, diff_lambda, diff_subln_g, diff_lam_init, w_out):
    B, S, _ = u.shape
    proj = u @ w_in
    offsets = [int(o) for o in np.cumsum([w for _, w in IN_SPLITS])[:-1]]
    p = dict(zip([n for n, _ in IN_SPLITS], jnp.split(proj, offsets, axis=-1)))

    def hd(name, *shape):
        return p[name].reshape(B, S, *shape)

    o_fox = fox_attention(hd('fox_q', FOX_HEADS, HEAD_DIM), hd('fox_k', FOX_HEADS, HEAD_DIM),
                          hd('fox_v', FOX_HEADS, HEAD_DIM), p['fox_f'], fox_f_bias)
    o_nsa = nsa_attention(hd('nsa_q', NSA_HEADS, HEAD_DIM),
                          hd('nsa_cmp_k', NSA_KV_GROUPS, HEAD_DIM), hd('nsa_cmp_v', NSA_KV_GROUPS, HEAD_DIM),
                          hd('nsa_slc_k', NSA_KV_GROUPS, HEAD_DIM), hd('nsa_slc_v', NSA_KV_GROUPS, HEAD_DIM),
                          hd('nsa_win_k', NSA_KV_GROUPS, HEAD_DIM), hd('nsa_win_v', NSA_KV_GROUPS, HEAD_DIM),
                          hd('nsa_gate', NSA_HEADS, 3), cmp_w1, cmp_w2, cmp_pos)
    o_dil = dilated_attention(hd('dil_q', DIL_HEADS, HEAD_DIM), hd('dil_k', DIL_HEADS, HEAD_DIM),
                              hd('dil_v', DIL_HEADS, HEAD_DIM))
    o_diff = diff_attention(hd('diff_q', DIFF_HEADS, 2, DIFF_QK_DIM), hd('diff_k', DIFF_HEADS, 2, DIFF_QK_DIM),
                            hd('diff_v', DIFF_HEADS, DIFF_V_DIM), diff_lambda, diff_subln_g, diff_lam_init)
    o = jnp.concatenate([o_fox, o_nsa, o_dil, o_diff], axis=-1)
    return o @ w_out


def swiglu(u, w_gate, w_up, w_down):
    return (jax.nn.silu(u @ w_gate) * (u @ w_up)) @ w_down


def setup_inputs(seed: int = 0) -> dict:
    key = jax.random.key(seed)
    ks = jax.random.split(key, 18)
    D = D_MODEL

    def nrm(k, shape, scale):
        return jax.random.normal(k, shape, jnp.float32) * scale

    return {
        'x': nrm(ks[0], (BATCH, SEQ, D), 1.0),
        'c': nrm(ks[1], (BATCH, D), 1.0),
        'ada_w': nrm(ks[2], (DEPTH, D, 6 * D), 0.5 * D ** -0.5),
        'ada_b': nrm(ks[3], (DEPTH, 6 * D), 0.02),
        'norm_mix_g': 1.0 + nrm(ks[4], (DEPTH, D), 0.02),
        'norm_ffn_g': 1.0 + nrm(ks[5], (DEPTH, D), 0.02),
        'w_in': nrm(ks[6], (DEPTH, D, N_IN), D ** -0.5),
        'fox_f_bias': 3.0 + nrm(ks[7], (DEPTH, FOX_HEADS), 0.5),
        'nsa_cmp_w1': nrm(ks[8], (DEPTH, 2, NSA_CMP_LEN * HEAD_DIM, HEAD_DIM), (NSA_CMP_LEN * HEAD_DIM) ** -0.5),
        'nsa_cmp_w2': nrm(ks[9], (DEPTH, 2, HEAD_DIM, HEAD_DIM), 1.5 * HEAD_DIM ** -0.5),
        'nsa_cmp_pos': nrm(ks[10], (DEPTH, 2, NSA_CMP_LEN, HEAD_DIM), 0.1),
        'diff_lambda': nrm(ks[11], (DEPTH, 4, DIFF_QK_DIM), 0.1),
        'diff_subln_g': 1.0 + nrm(ks[12], (DEPTH, DIFF_V_DIM), 0.02),
        'w_out': nrm(ks[13], (DEPTH, MIX_WIDTH, D), MIX_WIDTH ** -0.5),
        'ffn_w_gate': nrm(ks[14], (DEPTH, D, D_FF), D ** -0.5),
        'ffn_w_up': nrm(ks[15], (DEPTH, D, D_FF), D ** -0.5),
        'ffn_w_down': nrm(ks[16], (DEPTH, D_FF, D), D_FF ** -0.5),
        'final_norm_g': 1.0 + nrm(ks[17], (D,), 0.02),
    }


def reference(x, c, ada_w, ada_b, norm_mix_g, norm_ffn_g, w_in, fox_f_bias, nsa_cmp_w1, nsa_cmp_w2,
              nsa_cmp_pos, diff_lambda, diff_subln_g, w_out, ffn_w_gate, ffn_w_up, ffn_w_down, final_norm_g):
    c_act = jax.nn.silu(c)
    h = x
    for layer in range(DEPTH):
        mod = c_act @ ada_w[layer] + ada_b[layer]
        sh1, sc1, g1, sh2, sc2, g2 = [m[:, None, :] for m in jnp.split(mod, 6, axis=-1)]
        lam_init = 0.8 - 0.6 * math.exp(-0.3 * layer)
        u = rms_norm(h, norm_mix_g[layer]) * (1.0 + sc1) + sh1
        h = h + g1 * hybrid_mixer(u, w_in[layer], fox_f_bias[layer], nsa_cmp_w1[layer], nsa_cmp_w2[layer],
                                  nsa_cmp_pos[layer], diff_lambda[layer], diff_subln_g[layer], lam_init, w_out[layer])
        u = rms_norm(h, norm_ffn_g[layer]) * (1.0 + sc2) + sh2
        h = h + g2 * swiglu(u, ffn_w_gate[layer], ffn_w_up[layer], ffn_w_down[layer])
    return rms_norm(h, final_norm_g)
```

```python
import math
from contextlib import ExitStack

import numpy as np
import ml_dtypes
import concourse.bass as bass
import concourse.mybir as mybir
from concourse.bass_utils import run_bass_kernel_spmd

F32 = mybir.dt.float32
BF16 = mybir.dt.bfloat16
ALU = mybir.AluOpType
ACT = mybir.ActivationFunctionType
AX = mybir.AxisListType
NPBF = ml_dtypes.bfloat16

D = 2048
S = 4096
DEPTH = 4
DFF = 5632
NIN = 5919
NEG = -30000.0
EPS = 1e-6
ENGS = ("tensor", "vector", "scalar", "gpsimd", "sync")

OFF = dict(fox_q=0, fox_k=448, fox_v=896, fox_f=1344, nsa_q=1351, cmp_k=1863, cmp_v=1991, slc_k=2119,
           slc_v=2247, win_k=2375, win_v=2503, gate=2631, dil_q=2655, dil_k=3231, dil_v=3807,
           diff_q=4383, diff_k=4895, diff_v=5407)
Q_HEADS = [OFF['fox_q'] + 64 * h for h in range(7)] + [OFF['nsa_q'] + 64 * h for h in range(8)] + \
          [OFF['dil_q'] + 64 * h for h in range(9)] + [OFF['diff_q'] + 64 * h for h in range(8)]
K_HEADS = [OFF['fox_k'] + 64 * h for h in range(7)] + [OFF['cmp_k'] + 64 * g for g in range(2)] + \
          [OFF['cmp_v'] + 64 * g for g in range(2)] + [OFF['slc_k'] + 64 * g for g in range(2)] + \
          [OFF['win_k'] + 64 * g for g in range(2)] + [OFF['dil_k'] + 64 * h for h in range(9)] + \
          [OFF['diff_k'] + 64 * h for h in range(8)]
FM_COLS = np.concatenate([np.arange(o, o + 64) for o in Q_HEADS + K_HEADS])
TM_COLS = np.concatenate([np.arange(OFF['fox_v'], OFF['fox_v'] + 448), np.arange(OFF['slc_v'], OFF['slc_v'] + 128),
                          np.arange(OFF['win_v'], OFF['win_v'] + 128), np.arange(OFF['dil_v'], OFF['dil_v'] + 576),
                          np.arange(OFF['diff_v'], OFF['diff_v'] + 512), np.arange(OFF['fox_f'], OFF['fox_f'] + 7),
                          np.arange(OFF['gate'], OFF['gate'] + 24)])
KH_FOX, KH_CMPK, KH_CMPV, KH_SLC, KH_WIN, KH_DIL, KH_DIFF = 0, 7, 9, 11, 13, 15, 24
QH_FOX, QH_NSA, QH_DIL, QH_DIFF = 0, 7, 15, 24
VH_FOX, VH_SLC, VH_WIN, VH_DIL = 0, 7, 9, 11
AX_NSA, AX_DIL, AX_DIFF = 1, 9, 18
SLOPES = [0.0] + [2.0 ** (-8.0 * (h + 1) / 8) for h in range(8)] + [2.0 ** (-8.0 * (h + 1) / 9) for h in range(9)] + \
         [2.0 ** (-8.0 * (h + 1) / 4) for h in range(4)]
DIL_CFG = ((128, 1, 1), (512, 4, 4), (2048, 16, 16))
DIL_MOFF = (0, 5, 13)


def slot_of(kb):
    return (kb % 4) * 8 + kb // 4


def kb_of(s):
    return 4 * (s % 8) + s // 8


def split3(x):
    x = np.asarray(x, np.float64)
    a = x.astype(np.float32).astype(NPBF)
    r1 = x - a.astype(np.float64)
    b = r1.astype(np.float32).astype(NPBF)
    r2 = r1 - b.astype(np.float64)
    c = r2.astype(np.float32).astype(NPBF)
    return a, b, c


def cmp_to_sel_matrix(n_cmp, n_sel):
    a, b = 4, 2
    j = np.arange(n_sel)[:, None, None]
    idx = a * j + np.arange(a)[None, :, None] - np.arange(b)[None, None, :]
    jj = np.broadcast_to(j, idx.shape)
    ok = (idx >= 0) & (idx < n_cmp)
    m = np.zeros((n_cmp, n_sel), np.float32)
    np.add.at(m, (idx[ok], jj[ok]), 1.0)
    return m


def make_consts(r):
    c = {}
    tq = ((4 * np.arange(8)[:, None] + r) * 128 + np.arange(128)[None, :]).reshape(-1)
    qx = np.zeros((22, 3, 1024), NPBF)
    for a in range(1, 22):
        h3 = split3(-8.0 * SLOPES[a] * tq.astype(np.float64))
        for k in range(3):
            qx[a, k] = h3[k]
    c['qx'] = qx
    sl = np.arange(32)
    tk = (np.array([kb_of(s) for s in sl])[None, :] * 128 + np.arange(128)[:, None]).astype(np.float64)
    kbias = np.zeros((128, 22, 32), np.float32)
    for a in range(1, 22):
        kbias[:, a, :] = SLOPES[a] * tk
    c['kbias'] = kbias
    cprime = np.arange(2)[None, :] * 128 + np.arange(128)[:, None]
    cidx = np.vectorize(kb_of)(cprime // 8) * 8 + cprime % 8
    cend = 16 * cidx + 31
    kbc = np.zeros((128, 8, 2), np.float32)
    for h in range(8):
        kbc[:, h, :] = SLOPES[AX_NSA + h] * cend
    c['kbias_cmp'] = kbc
    kl = np.arange(128)[:, None]
    ql = np.arange(128)[None, :]
    camb = np.zeros((128, 4, 128), np.float32)
    for d in range(4):
        dist = (r - d) * 128 + ql - kl
        camb[:, d, :] = np.where(dist >= 0, 0.0, NEG)
    c['camb'] = camb.astype(NPBF)
    wm = np.zeros((128, 8, 128), np.float32)
    for d in range(8):
        dist = (r + 4 - d) * 128 + ql - kl
        wm[:, d, :] = np.where((dist >= 0) & (dist < 512), 0.0, NEG)
    c['wmask'] = wm.astype(NPBF)
    dm = np.zeros((128, 33, 128), np.float32)
    for g, (win, dil, nb) in enumerate(DIL_CFG):
        for d in range(nb + 4):
            dist = (r + nb - d) * 128 + ql - kl
            ok = (dist >= 0) & (dist % dil == 0) & (dist <= win)
            dm[:, DIL_MOFF[g] + d, :] = np.where(ok, 0.0, NEG)
    c['dmask'] = dm.astype(NPBF)
    cm = np.zeros((128, 2, 1024), np.float32)
    for blk in range(2):
        cc = cidx[:, blk][:, None]
        ok = (cc <= 254) & (16 * cc + 31 <= tq[None, :])
        cm[:, blk, :] = np.where(ok, 0.0, NEG)
    c['cmask'] = cm.astype(NPBF)
    n = np.arange(64)[None, None, :]
    tq3 = tq.reshape(8, 128).T[:, :, None]
    cur = tq3 // 64
    causal = (n * 64 <= tq3)
    forced = ((n == 0) | (n == cur) | (n == cur - 1)) & causal
    c['atab'] = (causal & ~forced).astype(np.float32)
    c['btab'] = np.where(forced, 10000.0 + n, np.where(causal, 0.0, -1e30)).astype(np.float32)
    E = np.zeros((64, 32, 128), np.float32)
    for s in range(32):
        for k in range(128):
            E[2 * kb_of(s) + k // 64, s, k] = 1.0
    c['esel'] = E.astype(NPBF)
    M = cmp_to_sel_matrix(255, 64)
    Mp = np.zeros((256, 64), np.float32)
    Mp[:255] = M
    c['mcs'] = Mp[cidx.reshape(-1)].reshape(128, 2, 64).astype(NPBF).copy()
    oh = np.zeros((128, 4), np.float32)
    oh[:, r] = 1.0
    c['onehot'] = oh
    s3 = np.zeros((128, 3), np.float32)
    for k in range(3):
        s3[64 + k, k] = 1.0
    c['sel3'] = s3
    return c


CONST_SPECS = [('qx', [22, 3, 1024], BF16), ('kbias', [128, 22, 32], F32), ('kbias_cmp', [128, 8, 2], F32),
               ('camb', [128, 4, 128], BF16), ('wmask', [128, 8, 128], BF16), ('dmask', [128, 33, 128], BF16),
               ('cmask', [128, 2, 1024], BF16), ('atab', [128, 8, 64], F32), ('btab', [128, 8, 64], F32),
               ('esel', [64, 32, 128], BF16), ('mcs', [128, 2, 64], BF16), ('onehot', [128, 4], F32), ('sel3', [128, 3], F32)]


class Prog:
    def __init__(self, nc, n_dma_sems=90):
        self.nc = nc
        self.stack = ExitStack()
        self.q = {e: [] for e in ENGS}
        self.cnt = {e: 0 for e in ENGS}
        self.seen = {e: {} for e in ENGS}
        self.esem = {e: self.stack.enter_context(nc.semaphore("e_" + e)) for e in ENGS}
        self.dma_pool = [self.stack.enter_context(nc.semaphore("d%d" % i)) for i in range(n_dma_sems)]
        self.dma_sem_of = {}
        self.dma_total = {}
        self.last_w = {}
        self.readers = {}
        self.nsb = 0
        self.ninst = 0

    def _need(self, eng, tok, waits):
        if tok is None:
            return
        sem, val, src = tok
        if src == eng and eng in ("tensor",):
            return
        sid = id(sem)
        if src is None:
            val = max(val, self.dma_total.get(sid, val))
        if self.seen[eng].get(sid, 0) >= val:
            return
        self.seen[eng][sid] = val
        waits.append((sem, val))

    def _deps(self, eng, reads, writes):
        waits = []
        for k in reads:
            self._need(eng, self.last_w.get(k), waits)
        for k in writes:
            self._need(eng, self.last_w.get(k), waits)
            for t in self.readers.get(k, ()):
                self._need(eng, t, waits)
        if len(waits) > 1:
            best = {}
            for sem, val in waits:
                if id(sem) not in best or best[id(sem)][1] < val:
                    best[id(sem)] = (sem, val)
            waits = list(best.values())
        return waits

    def _commit(self, tok, reads, writes):
        for k in writes:
            self.last_w[k] = tok
            self.readers[k] = []
        for k in reads:
            if k in writes:
                continue
            lst = self.readers.setdefault(k, [])
            lst.append(tok)
            if len(lst) > 48:
                best = {}
                for t in lst:
                    b = best.get(id(t[0]))
                    if b is None or t[1] > b[1]:
                        best[id(t[0])] = t
                self.readers[k] = list(best.values())

    def op(self, eng, fn, reads=(), writes=()):
        waits = self._deps(eng, reads, writes)
        self.cnt[eng] += 1
        tok = (self.esem[eng], self.cnt[eng], eng)
        self.q[eng].append((waits, fn, (self.esem[eng], 1)))
        self._commit(tok, reads, writes)
        self.ninst += 1
        return tok

    def dma(self, eng, fn, reads=(), writes=(), semkey=None):
        waits = self._deps(eng, reads, writes)
        if semkey is None:
            semkey = writes[0] if writes else reads[0]
        if semkey not in self.dma_sem_of:
            if not self.dma_pool:
                raise RuntimeError("out of DMA semaphores")
            self.dma_sem_of[semkey] = [self.dma_pool.pop(), 0]
        ent = self.dma_sem_of[semkey]
        ent[1] += 16
        self.dma_total[id(ent[0])] = ent[1]
        tok = (ent[0], ent[1], None)
        self.q[eng].append((waits, fn, (ent[0], 16)))
        self._commit(tok, reads, writes)
        self.ninst += 1
        return tok

    def barrier(self):
        for eng in ENGS:
            waits = []
            for e in ENGS:
                if self.cnt[e] and e != eng:
                    self._need(eng, (self.esem[e], self.cnt[e], e), waits)
            for k, (sem, v) in self.dma_sem_of.items():
                if v:
                    self._need(eng, (sem, v, None), waits)
            if waits:
                self.q[eng].append((waits, None, None))

    def emit(self):
        nc = self.nc
        with nc.Block() as block:
            for e in ENGS:
                items = self.q[e]

                def body(engobj, items=items):
                    for waits, fn, inc in items:
                        for sem, val in waits:
                            engobj.wait_ge(sem, val)
                        if fn is not None:
                            ins = fn(engobj)
                            if inc is not None:
                                ins.then_inc(inc[0], inc[1])

                getattr(block, e)(body)
        self.q = {e: [] for e in ENGS}

    def sb(self, stack, shape, dt, name=None):
        self.nsb += 1
        return stack.enter_context(self.nc.sbuf_tensor(name or ("sb%d" % self.nsb), list(shape), dt))

    def ps(self, stack, shape, dt=F32, name=None):
        self.nsb += 1
        return stack.enter_context(self.nc.psum_tensor(name or ("ps%d" % self.nsb), list(shape), dt))


class Builder:
    def __init__(self, has_b, has_a, final, debug=False, att_only=False):
        self.has_b, self.has_a, self.final, self.debug = has_b, has_a, final, debug
        self.att_only = att_only
        self.nc = bass.Bass("TRN2", target_bir_lowering=False)
        self.P = Prog(self.nc)
        self.dram = {}
        self.rr = 0

    def din(self, name, shape, dt):
        self.dram[name] = self.nc.dram_tensor(name, list(shape), dt, kind="ExternalInput").ap()
        return self.dram[name]

    def dout(self, name, shape, dt):
        self.dram[name] = self.nc.dram_tensor(name, list(shape), dt, kind="ExternalOutput").ap()
        return self.dram[name]

    def evac_engine(self):
        self.rr += 1
        return "scalar" if self.rr % 2 else "vector"

    def copy(self, eng, out, in_, reads, writes):
        if eng == "scalar":
            self.P.op("scalar", lambda e: e.activation(out=out, in_=in_, func=ACT.Copy), reads, writes)
        else:
            self.P.op(eng, lambda e: e.tensor_copy(out=out, in_=in_), reads, writes)

    def load_w(self, slot, W, KC, f0, G):
        P = self.P
        wb = self.wbuf[slot]
        Wv = W.rearrange("(kc p) f -> p kc f", p=128)
        keys = []
        step = 8
        for k0 in range(0, KC, step):
            k1 = min(KC, k0 + step)
            key = ("wbuf", slot, k0 // step)
            P.dma("gpsimd", lambda e, k0=k0, k1=k1: e.dma_start(out=wb[:, k0:k1, 0:G], in_=Wv[:, k0:k1, f0:f0 + G]),
                  writes=[key], semkey=("wbufsem", slot))
            keys.append(key)
        return keys

    def fm_linear(self, W, KC, F, act_fn, act_keys, ntb, tokw, evac):
        P = self.P
        G = 256
        ng = (F + G - 1) // G
        keys = self.load_w(self.wslot, W, KC, 0, min(G, F))
        for g in range(ng):
            slot = self.wslot
            cur_keys = keys
            self.wslot ^= 1
            if g + 1 < ng:
                keys = self.load_w(self.wslot, W, KC, (g + 1) * G, min(G, F - (g + 1) * G))
            gw = min(G, F - g * G)
            for fi in range(gw // 128):
                ft = g * (G // 128) + fi
                for tb in range(ntb):
                    pb = self.pbank()
                    pkey = ("pb", pb)
                    ps = self.PB[pb]
                    for kc in range(KC):
                        P.op("tensor", lambda e, kc=kc, tb=tb, fi=fi, slot=slot, ps=ps: e.matmul(
                            ps[:, 0:tokw], lhsT=self.wbuf[slot][:, kc, fi * 128:(fi + 1) * 128], rhs=act_fn(kc, tb),
                            start=(kc == 0), stop=(kc == KC - 1)),
                             reads=cur_keys + act_keys, writes=[pkey])
                    evac(ft, tb, ps[:, 0:tokw], pkey)

    def alloc_w(self, st):
        self.wbuf = [self.P.sb(st, [128, 44, 256], BF16) for i in range(2)]

    def pbank(self):
        self.pb_rr = (self.pb_rr + 1) % 6
        return self.pb_rr

    def build(self):
        nc, P = self.nc, self.P
        top = ExitStack()
        self.top = top
        cd = {n: self.din("c_" + n, s, dt) for n, s, dt in CONST_SPECS}
        hin = self.din("h_in", [128, 16, 1024], F32)
        cT_d = self.din("cT", [128, 16], F32)
        if self.has_b:
            kTg = self.din("kTg", [4, 32, 64, 1024], BF16)
            v64g = self.din("v64g", [4, 20, 128, 512], BF16)
            v128g = self.din("v128g", [4, 4, 128, 1024], BF16)
            fg = self.din("fg", [4, 128, 8, 32], F32)
            fown = self.din("fown", [128, 8, 32], F32)
            qT_in = self.din("qT_in", [32, 64, 1024], BF16)
            modB = self.din("modB", [128, 96], F32)
            if not self.att_only:
                w_out = self.din("w_out", [D, D], F32)
                w_gate = self.din("w_gate", [D, DFF], F32)
                w_up = self.din("w_up", [D, DFF], F32)
                w_down = self.din("w_down", [DFF, D], F32)
                gffn = self.din("gffn", [128, 16], F32)
            foxb = self.din("foxb", [128, 7], F32)
            cw1 = self.din("cw1", [2, 64, 32, 64], F32)
            cw2 = self.din("cw2", [2, 64, 64], F32)
            cposT = self.din("cposT", [2, 64, 32], F32)
            dlam = self.din("dlam", [128, 4, 64], F32)
            dsub = self.din("dsub", [128, 128], F32)
            lami = self.din("lami", [128, 2], F32)
        if self.has_a:
            if not self.has_b:
                ada_w = self.din("ada_w", [D, 6 * D], F32)
                ada_b = self.din("ada_b", [128, 96], F32)
                ada_wx = self.din("ada_wx", [3, D, 1536], F32)
                ada_bx = self.din("ada_bx", [128, 3, 12], F32)
                cT2_d = self.din("cT2", [128, 16, 2], F32)
                modx_o = self.dout("modx_o", [128, 3, 12, 2], F32)
                mod_o = self.dout("mod_o", [128, 96], F32)
            else:
                modA = self.din("modA", [128, 96], F32)
            gmix = self.din("gmix", [128, 16], F32)
            w_fm = self.din("w_fm", [D, 4096], F32)
            w_tm = self.din("w_tm", [D, 1824], F32)
            qT_o = self.dout("qT_o", [32, 64, 1024], BF16)
            kT_o = self.dout("kT_o", [32, 64, 1024], BF16)
            v64_o = self.dout("v64_o", [20, 128, 512], BF16)
            v128_o = self.dout("v128_o", [4, 128, 1024], BF16)
            f_o = self.dout("f_o", [128, 8, 32], F32)
        if self.final:
            gfin = self.din("gfin", [128, 16], F32)
        hout = self.dout("h_out", [128, 16, 1024], F32) if self.has_b else None
        if self.debug:
            dbg_o = self.dout("dbg_o", [128, 8, 2048], BF16)

        self.uT = P.sb(top, [128, 16, 1024], BF16, "uT")
        self.wslot = 0
        self.PB = [P.ps(top, [128, 512], F32, "pb%d" % i) for i in range(7)]
        self.PT16 = P.ps(top, [128, 1024], BF16, "pt16")
        self.pb_rr = 0
        ident = P.sb(top, [128, 128], BF16, "ident")
        ones = P.sb(top, [128, 128], BF16, "ones")
        epsb = P.sb(top, [128, 1], F32, "epsb")
        cact = P.sb(top, [128, 16], BF16, "cact")
        cT = P.sb(top, [128, 16], F32, "cT_sb")
        mod = P.sb(top, [128, 96], F32, "mod_sb")
        self.ident, self.ones, self.epsb = ident, ones, epsb
        uT = self.uT

        P.op("gpsimd", lambda e: e.memset(ident[:], 1.0), writes=["ident"])
        P.op("gpsimd", lambda e: e.affine_select(out=ident[:], in_=ident[:], pattern=[[-1, 128]],
                                                  compare_op=ALU.is_equal, fill=0.0, base=0, channel_multiplier=1),
             reads=["ident"], writes=["ident"])
        P.op("gpsimd", lambda e: e.memset(ones[:], 1.0), writes=["ones"])
        P.op("gpsimd", lambda e: e.memset(epsb[:], EPS), writes=["epsb"])
        P.dma("sync", lambda e: e.dma_start(out=cT[:], in_=cT_d), writes=["cT"])
        P.op("scalar", lambda e: e.activation(out=cact[:], in_=cT[:], func=ACT.Silu), reads=["cT"], writes=["cact"])

        if self.has_b:
            P.dma("sync", lambda e: e.dma_start(out=mod[:], in_=modB), writes=["mod"])
            self.attention(cd, kTg, v64g, v128g, fg, fown, qT_in, foxb, cw1, cw2, cposT, dlam, dsub, lami,
                           dbg_o if self.debug else None)
        self.hT = P.sb(top, [128, 16, 1024], F32, "hT")
        hT = self.hT
        for c4 in range(4):
            P.dma("sync", lambda e, c4=c4: e.dma_start(out=hT[:, 4 * c4:4 * c4 + 4, :], in_=hin[:, 4 * c4:4 * c4 + 4, :]),
                  writes=[("hT", c) for c in range(4 * c4, 4 * c4 + 4)], semkey=("hTld", c4))
        def store_h():
            for c4 in range(4):
                P.dma("sync", lambda e, c4=c4: e.dma_start(out=hout[:, 4 * c4:4 * c4 + 4, :], in_=hT[:, 4 * c4:4 * c4 + 4, :]),
                      reads=[("hT", c) for c in range(4 * c4, 4 * c4 + 4)], writes=[("hout", c4)])

        if self.has_b and not self.att_only:
            self.out_proj(w_out, mod)
            self.ffn(mod, gffn, w_gate, w_up, w_down)
            if not self.final:
                store_h()
        if self.has_a:
            if not self.has_b:
                self.mod_phase(ada_w, ada_b, cact, mod, mod_o)
                self.modx_phase(ada_wx, ada_bx, cT2_d, modx_o)
            else:
                P.dma("sync", lambda e: e.dma_start(out=mod[:], in_=modA), writes=["mod"])
            self.proj_phase(mod, gmix, w_fm, w_tm, qT_o, kT_o, v64_o, v128_o, f_o)
            if self.debug and not self.has_b:
                P.dma("sync", lambda e: e.dma_start(out=dbg_o.rearrange("p a (b t) -> p (a b) t", b=2), in_=uT[:]),
                      reads=[("uT", c, tb) for c in range(16) for tb in range(2)], writes=["dbg"])
        if self.final:
            self.final_phase(gfin, hout)
        elif self.has_b and self.att_only:
            store_h()
        P.barrier()
        P.emit()
        top.close()
        P.stack.close()
        return nc

    def norm_phase(self, st, avec, bvec, akey, tbs=(0, 1)):
        P = self.P
        hT, uT = self.hT, self.uT
        sq = [P.sb(st, [128, 512], BF16) for _ in range(2)]
        rstd0 = P.sb(st, [128, 512], F32)
        rstd = P.sb(st, [128, 512], F32)
        tmp = [P.sb(st, [128, 512], F32) for _ in range(2)]
        for tb in tbs:
            tsl = slice(tb * 512, (tb + 1) * 512)
            pb = self.pbank()
            ps = self.PB[pb]
            for c in range(16):
                s_ = sq[c % 2]
                sk = ("sq", id(s_))
                P.op("scalar", lambda e, tsl=tsl, c=c, s_=s_: e.activation(out=s_[:], in_=hT[:, c, tsl], func=ACT.Square),
                     reads=[("hT", c)], writes=[sk])
                P.op("tensor", lambda e, c=c, s_=s_, ps=ps: e.matmul(ps[:], lhsT=self.ones[:], rhs=s_[:], start=(c == 0), stop=(c == 15)),
                     reads=[sk, "ones"], writes=[("pb", pb)])
            rk = ("rstd", id(rstd))
            rk0 = ("rstd0", id(rstd0))
            P.op("scalar", lambda e, ps=ps: e.activation(out=rstd0[:], in_=ps[:], func=ACT.Sqrt, bias=self.epsb[:], scale=1.0 / D),
                 reads=[("pb", pb), "epsb"], writes=[rk0])
            P.op("vector", lambda e: e.reciprocal(out=rstd[:], in_=rstd0[:]), reads=[rk0], writes=[rk])
            for c in range(16):
                t_ = tmp[c % 2]
                tk = ("ntmp", id(t_))
                P.op("vector", lambda e, tsl=tsl, c=c, t_=t_: e.tensor_tensor(out=t_[:], in0=hT[:, c, tsl], in1=rstd[:], op=ALU.mult),
                     reads=[("hT", c), rk], writes=[tk])
                P.op("scalar", lambda e, tsl=tsl, c=c, t_=t_: e.activation(out=uT[:, c, tsl], in_=t_[:], func=ACT.Identity,
                                                                   bias=bvec[:, c:c + 1], scale=avec[:, c:c + 1]),
                     reads=[tk, akey, "mod"], writes=[("uT", c, tb)])

    def make_ab(self, st, gd, mod, sc_col, key):
        P = self.P
        g = P.sb(st, [128, 16], F32)
        a = P.sb(st, [128, 16], F32)
        P.dma("sync", lambda e: e.dma_start(out=g[:], in_=gd), writes=[(key, "g")])
        P.op("vector", lambda e: e.tensor_scalar(out=a[:], in0=mod[:, sc_col:sc_col + 16], scalar1=1.0, scalar2=None, op0=ALU.add),
             reads=["mod"], writes=[key])
        P.op("vector", lambda e: e.tensor_tensor(out=a[:], in0=a[:], in1=g[:], op=ALU.mult), reads=[key, (key, "g")], writes=[key])
        return a

    def mod_phase(self, ada_w, ada_b, cact, mod, mod_o):
        P = self.P
        P.barrier()
        with ExitStack() as st:
            self.alloc_w(st)
            bt = P.sb(st, [128, 96], F32)
            P.dma("sync", lambda e: e.dma_start(out=bt[:], in_=ada_b), writes=["adab"])

            def evac(ft, tb, ps, pkey):
                P.op("vector", lambda e: e.tensor_tensor(out=mod[:, ft:ft + 1], in0=ps, in1=bt[:, ft:ft + 1], op=ALU.add),
                     reads=[pkey, "adab"], writes=["mod"])

            self.fm_linear(ada_w, 16, 6 * D, lambda kc, tb: cact[:, kc:kc + 1], ["cact"], 1, 1, evac)
            P.dma("sync", lambda e: e.dma_start(out=mod_o, in_=mod[:]), reads=["mod"], writes=["mod_o"])
            P.barrier()
            P.emit()

    def modx_phase(self, ada_wx, ada_bx, cT2_d, modx_o):
        P = self.P
        with ExitStack() as st:
            self.alloc_w(st)
            bt = P.sb(st, [128, 3, 12], F32)
            c2 = P.sb(st, [128, 16, 2], F32)
            ca2 = P.sb(st, [128, 16, 2], BF16)
            mx = P.sb(st, [128, 3, 12, 2], F32)
            P.dma("sync", lambda e: e.dma_start(out=bt[:], in_=ada_bx), writes=["adabx"])
            P.dma("sync", lambda e: e.dma_start(out=c2[:], in_=cT2_d), writes=["c2"])
            P.op("scalar", lambda e: e.activation(out=ca2[:], in_=c2[:], func=ACT.Silu), reads=["c2"], writes=["ca2"])
            for lyr in range(3):
                def evac(ft, tb, ps, pkey, lyr=lyr):
                    P.op("vector", lambda e: e.tensor_scalar(out=mx[:, lyr, ft, :], in0=ps, scalar1=bt[:, lyr, ft:ft + 1], scalar2=None, op0=ALU.add),
                         reads=[pkey, "adabx"], writes=["mx"])

                self.fm_linear(ada_wx[lyr], 16, 1536, lambda kc, tb: ca2[:, kc, :], ["ca2"], 1, 2, evac)
            P.dma("sync", lambda e: e.dma_start(out=modx_o, in_=mx[:]), reads=["mx"], writes=["modx_o"])
            P.barrier()
            P.emit()

    def proj_phase(self, mod, gmix, w_fm, w_tm, qT_o, kT_o, v64_o, v128_o, f_o):
        P = self.P
        uT = self.uT
        with ExitStack() as st:
            self.alloc_w(st)
            a1 = self.make_ab(st, gmix, mod, 16, "a1")
            self.norm_phase(st, a1, mod[:, 0:16], "a1")
            stq = [P.sb(st, [128, 1024], BF16) for _ in range(2)]
            stv = P.sb(st, [128, 8, 1792], BF16)
            stf = P.sb(st, [128, 8, 32], F32)
            ukeys = [("uT", c, tb) for c in range(16) for tb in range(2)] + ["mod"]

            def evac(ft, tb, ps, pkey):
                sq_ = stq[ft % 2]
                eng = self.evac_engine()
                self.copy(eng, sq_[:, tb * 512:(tb + 1) * 512], ps, [pkey], [("stq", ft % 2, tb)])
                if tb == 1:
                    for hh in range(2):
                        head = 2 * ft + hh
                        dst = qT_o[head] if head < 32 else kT_o[head - 32]
                        P.dma("sync", lambda e, dst=dst, hh=hh, sq_=sq_: e.dma_start(out=dst, in_=sq_[hh * 64:(hh + 1) * 64, :]),
                              reads=[("stq", ft % 2, 0), ("stq", ft % 2, 1)], writes=[("qko", head)], semkey=("stqsem", ft % 2, hh))

            self.fm_linear(w_fm, 16, 4096, lambda kc, tb: uT[:, kc, tb * 512:(tb + 1) * 512], ukeys, 2, 512, evac)
            P.op("gpsimd", lambda e: e.memset(stf[:], 0.0), writes=["stf"])
            groups = [(0, 512), (512, 512), (1024, 512), (1536, 288)]
            Wv = w_tm.rearrange("(kc p) f -> p kc f", p=128)
            for gi, (f0, gw) in enumerate(groups):
                wkeys = []
                for hh in range(0, gw, 256):
                    hw = min(256, gw - hh)
                    slot = self.wslot
                    self.wslot ^= 1
                    wkeys.append((slot, hh, hw, self.load_w(slot, w_tm, 16, f0 + hh, hw)))
                for i in range(8):
                    for (slot, hh, hw, keys) in wkeys:
                        pb = self.pbank()
                        ps = self.PB[pb]
                        for kc in range(16):
                            P.op("tensor", lambda e, kc=kc, i=i, slot=slot, hw=hw, ps=ps: e.matmul(
                                ps[:, 0:hw], lhsT=uT[:, kc, i * 128:(i + 1) * 128], rhs=self.wbuf[slot][:, kc, 0:hw],
                                start=(kc == 0), stop=(kc == 15)), reads=keys + ukeys, writes=[("pb", pb)])
                        c0 = f0 + hh
                        eng = self.evac_engine()
                        if c0 + hw <= 1792:
                            self.copy(eng, stv[:, i, c0:c0 + hw], ps[:, 0:hw], [("pb", pb)], [("stv", c0)])
                        else:
                            nv = 1792 - c0
                            assert nv == 0
                            self.copy("vector", stf[:, i, 0:31], ps[:, nv:nv + 31], [("pb", pb)], ["stf"])
            allv = [("stv", c0) for c0 in (0, 256, 512, 768, 1024, 1280, 1536)]
            for hh in range(20):
                P.dma("sync", lambda e, hh=hh: e.dma_start(out=v64_o[hh].rearrange("p (i d) -> p i d", d=64), in_=stv[:, :, hh * 64:(hh + 1) * 64]),
                      reads=allv, writes=[("v64o", hh)], semkey="vout")
            for hh in range(4):
                P.dma("sync", lambda e, hh=hh: e.dma_start(out=v128_o[hh].rearrange("p (i d) -> p i d", d=128),
                                                            in_=stv[:, :, 1280 + hh * 128:1280 + (hh + 1) * 128]),
                      reads=allv, writes=[("v128o", hh)], semkey="vout")
            P.dma("sync", lambda e: e.dma_start(out=f_o, in_=stf[:]), reads=["stf"], writes=["f_o"], semkey="vout")
            P.barrier()
            P.emit()

    def out_proj(self, w_out, mod):
        P = self.P
        hT, uT = self.hT, self.uT
        with ExitStack() as st:
            self.alloc_w(st)
            ukeys = [("uT", c, tb) for c in range(16) for tb in range(2)]

            def evac(ft, tb, ps, pkey):
                tsl = slice(tb * 512, (tb + 1) * 512)
                P.op("vector", lambda e: e.scalar_tensor_tensor(out=hT[:, ft, tsl], in0=ps, scalar=mod[:, 32 + ft:33 + ft],
                                                                in1=hT[:, ft, tsl], op0=ALU.mult, op1=ALU.add),
                     reads=[pkey, "mod", ("hT", ft)], writes=[("hT", ft)])

            self.fm_linear(w_out, 16, D, lambda kc, tb: uT[:, kc, tb * 512:(tb + 1) * 512], ukeys, 2, 512, evac)
            P.barrier()
            P.emit()

    def ffn(self, mod, gffn, w_gate, w_up, w_down):
        P = self.P
        hT, uT = self.hT, self.uT
        HF = DFF // 2
        with ExitStack() as st:
            self.alloc_w(st)
            a2 = self.make_ab(st, gffn, mod, 64, "a2")
            hid = P.sb(st, [128, 22, 1024], BF16)
            self.norm_phase(st, a2, mod[:, 48:64], "a2")
            ukeys = [("uT", c, tb) for c in range(16) for tb in range(2)]
            for half in range(2):
                def evac_g(ft, tb, ps, pkey):
                    P.op("scalar", lambda e: e.activation(out=hid[:, ft, tb * 512:(tb + 1) * 512], in_=ps, func=ACT.Silu),
                         reads=[pkey], writes=[("hid", ft, tb)])

                def evac_u(ft, tb, ps, pkey):
                    P.op("vector", lambda e: e.tensor_tensor(out=hid[:, ft, tb * 512:(tb + 1) * 512], in0=ps,
                                                             in1=hid[:, ft, tb * 512:(tb + 1) * 512], op=ALU.mult),
                         reads=[pkey, ("hid", ft, tb)], writes=[("hid", ft, tb)])

                self.fm_linear(w_gate[:, half * HF:(half + 1) * HF], 16, HF, lambda kc, tb: uT[:, kc, tb * 512:(tb + 1) * 512], ukeys, 2, 512, evac_g)
                self.fm_linear(w_up[:, half * HF:(half + 1) * HF], 16, HF, lambda kc, tb: uT[:, kc, tb * 512:(tb + 1) * 512], ukeys, 2, 512, evac_u)
                hkeys = [("hid", ft, tb) for ft in range(22) for tb in range(2)]

                def evac_d(ft, tb, ps, pkey):
                    tsl = slice(tb * 512, (tb + 1) * 512)
                    P.op("vector", lambda e: e.scalar_tensor_tensor(out=hT[:, ft, tsl], in0=ps, scalar=mod[:, 80 + ft:81 + ft],
                                                                    in1=hT[:, ft, tsl], op0=ALU.mult, op1=ALU.add),
                         reads=[pkey, "mod", ("hT", ft)], writes=[("hT", ft)])

                self.fm_linear(w_down[half * HF:(half + 1) * HF, :], 22, D, lambda kc, tb: hid[:, kc, tb * 512:(tb + 1) * 512], hkeys, 2, 512, evac_d)
            P.barrier()
            P.emit()

    def final_phase(self, gfin, hout):
        P = self.P
        with ExitStack() as st:
            g = P.sb(st, [128, 16], F32)
            P.dma("sync", lambda e: e.dma_start(out=g[:], in_=gfin), writes=["gfin"])
            hT = self.hT
            sq = [P.sb(st, [128, 512], BF16) for _ in range(2)]
            rstd = P.sb(st, [128, 512], F32)
            for tb in range(2):
                tsl = slice(tb * 512, (tb + 1) * 512)
                pb = self.pbank()
                ps = self.PB[pb]
                for c in range(16):
                    s_ = sq[c % 2]
                    sk = ("sq", id(s_))
                    P.op("scalar", lambda e, tsl=tsl, c=c, s_=s_: e.activation(out=s_[:], in_=hT[:, c, tsl], func=ACT.Square),
                         reads=[("hT", c)], writes=[sk])
                    P.op("tensor", lambda e, c=c, s_=s_, ps=ps: e.matmul(ps[:], lhsT=self.ones[:], rhs=s_[:], start=(c == 0), stop=(c == 15)),
                         reads=[sk, "ones"], writes=[("pb", pb)])
                rk = ("rstdf", tb)
                P.op("scalar", lambda e, ps=ps: e.activation(out=rstd[:], in_=ps[:], func=ACT.Sqrt, bias=self.epsb[:], scale=1.0 / D),
                     reads=[("pb", pb), "epsb", ("rstdf", 1 - tb)], writes=[rk])
                P.op("vector", lambda e: e.reciprocal(out=rstd[:], in_=rstd[:]), reads=[rk], writes=[rk])
                for c in range(16):
                    P.op("vector", lambda e, tsl=tsl, c=c: e.tensor_tensor(out=hT[:, c, tsl], in0=hT[:, c, tsl], in1=rstd[:], op=ALU.mult),
                         reads=[("hT", c), rk], writes=[("hT", c)])
                    P.op("vector", lambda e, tsl=tsl, c=c: e.tensor_scalar(out=hT[:, c, tsl], in0=hT[:, c, tsl], scalar1=g[:, c:c + 1], scalar2=None, op0=ALU.mult),
                         reads=[("hT", c), "gfin"], writes=[("hT", c)])
            for c4 in range(4):
                P.dma("sync", lambda e, c4=c4: e.dma_start(out=hout[:, 4 * c4:4 * c4 + 4, :], in_=hT[:, 4 * c4:4 * c4 + 4, :]),
                      reads=[("hT", c) for c in range(4 * c4, 4 * c4 + 4)], writes=[("hout", c4)])
            P.barrier()
            P.emit()


ATT_MODE = "full"
DBG = {}


def _on(name):
    return ATT_MODE == "full" or name in ATT_MODE.split(",")


def _attention(self, cd, kTg, v64g, v128g, fg, fown, qT_in, foxb, cw1, cw2, cposT, dlam, dsub, lami, dbg_o):
    P = self.P
    PB = self.PB
    ident = self.ident
    with ExitStack() as st:
        def ctile(name, shape, dt):
            t = P.sb(st, shape, dt, "k_" + name)
            P.dma("sync", lambda e: e.dma_start(out=t[:], in_=cd[name]), writes=[("c", name)], semkey=("constld", name))
            return t
        camb = ctile('camb', [128, 4, 128], BF16)
        wmask = ctile('wmask', [128, 8, 128], BF16)
        dmask = ctile('dmask', [128, 33, 128], BF16)
        cmask = ctile('cmask', [128, 2, 1024], BF16)
        kbias = ctile('kbias', [128, 22, 32], F32)
        kbias_cmp = ctile('kbias_cmp', [128, 8, 2], F32)
        atab = ctile('atab', [128, 8, 64], F32)
        btab = ctile('btab', [128, 8, 64], F32)
        esel = ctile('esel', [64, 32, 128], BF16)
        onehot = ctile('onehot', [128, 4], F32)
        sel3 = ctile('sel3', [128, 3], F32)
        CK = [("c", n) for n in ('camb', 'wmask', 'dmask', 'cmask', 'kbias', 'kbias_cmp', 'atab', 'btab', 'esel', 'onehot', 'sel3')]

        otm = P.sb(st, [128, 8, 2048], BF16, "otm")
        if ATT_MODE != "full" or DBG:
            for i in range(8):
                P.op("gpsimd", lambda e, i=i: e.memset(otm[:, i, :], 0.0), writes=[("otm", i)])
        Kb = [P.sb(st, [67, 4096], BF16, "Kb%d" % i) for i in range(2)]
        Qb = [P.sb(st, [67, 1024], BF16, "Qb%d" % i) for i in range(2)]
        V64 = [P.sb(st, [128, 32, 72], BF16, "V64_%d" % i) for i in range(2)]
        V128 = [P.sb(st, [128, 32, 136], BF16, "V128_0")] * 2
        pts = [P.sb(st, [128, 128], BF16, "pt%d" % i) for i in range(8)]
        gate = P.sb(st, [128, 8, 24], F32, "gate")
        small = P.sb(st, [128, 64], F32, "small")
        sm_i = [0]

        def sm(n=1):
            a = sm_i[0] % 64
            if a + n > 64:
                a = 0
            sm_i[0] = a + n
            return small[:, a:a + n], ("small", a // 1)

        for i in range(2):
            P.op("gpsimd", lambda e, i=i: e.memset(Kb[i][64:67, :], 1.0), writes=[("Kb1", i)])
            P.op("gpsimd", lambda e, i=i: e.memset(V64[i][:, :, 64:65], 1.0), writes=[("V1", 64, i)])
            P.op("gpsimd", lambda e, i=i: e.memset(V128[i][:, :, 128:129], 1.0), writes=[("V1", 128, 0)])
        P.dma("sync", lambda e: e.dma_start(out=gate[:], in_=fown[:, :, 7:31]), writes=["gate"])
        P.op("scalar", lambda e: e.activation(out=gate[:], in_=gate[:], func=ACT.Sigmoid), reads=["gate"], writes=["gate"])

        st_ = dict(k=0, q=0, v64=0, v128=0, o=0, s=0, p=0, cs=0, cp=0)

        def load_k(kh):
            j = st_['k']; st_['k'] ^= 1
            P.dma("sync", lambda e: e.dma_start(out=Kb[j][0:64, :].rearrange("d (r t) -> d r t", r=4),
                                                in_=kTg[:, kh].rearrange("r d t -> d r t")),
                  writes=[("Kb", j)], semkey=("Kbsem", j))
            return j

        def load_q(qh, ax):
            j = st_['q']; st_['q'] ^= 1
            P.dma("sync", lambda e: e.dma_start(out=Qb[j][0:64, :], in_=qT_in[qh]), writes=[("Qb", j)], semkey=("Qbsem", j))
            P.dma("sync", lambda e: e.dma_start(out=Qb[j][64:67, :], in_=cd['qx'][ax]), writes=[("Qx", j)], semkey=("Qbsem", j))
            return j

        def load_v(vh, dv):
            if dv == 64:
                j = st_['v64']; st_['v64'] ^= 1
                for r in range(4):
                    P.dma("sync", lambda e, r=r: e.dma_start(out=V64[j][:, r * 8:(r + 1) * 8, 0:64],
                                                              in_=v64g[r, vh].rearrange("p (i d) -> p i d", d=64)),
                          writes=[("V", 64, j, r)], semkey=("Vsem", 64, j))
                return j
            j = 0
            for r in range(4):
                P.dma("sync", lambda e, r=r: e.dma_start(out=V128[j][:, r * 8:(r + 1) * 8, 0:128],
                                                          in_=v128g[r, vh].rearrange("p (i d) -> p i d", d=128)),
                      writes=[("V", 128, j, r)], semkey=("Vsem", 128, j))
            return j

        def S_ap(n):
            return PB[n][:, 0:128]

        def run_head(kap, kkeys, qap, qkeys, v_fn, vkeys, ow, pairs, bias_fn, finalize, krows=67):
            flat = []
            for i, lst in pairs:
                if i >= DBG.get("maxt", 8):
                    continue
                if DBG.get("nomask"):
                    lst = [(a, b, []) for (a, b, c_) in lst]
                for n, (kc, vs, masks) in enumerate(lst):
                    flat.append((i, kc, vs, masks, n == 0, n == len(lst) - 1))
            LAG = DBG.get("blag", 4)
            N = len(flat)
            oslot = {}
            for n in range(N + LAG):
                if n < N:
                    i, kc, vs, masks, first, last = flat[n]
                    NS = DBG.get("bslots", 5)
                    sl = (0, 1, 2, 3, 6)[st_['s'] % NS]; st_['s'] = (st_['s'] + 1) % NS
                    pl = st_['p']; st_['p'] = (pl + 1) % 8
                    flat[n] = flat[n] + (pl,)
                    sap = S_ap(sl)
                    pem = [mm for mm in masks if mm[0] is not None]
                    dvm = [mm for mm in masks if mm[0] is None]
                    nm = len(pem)
                    P.op("tensor", lambda e, sap=sap, kc=kc, i=i, nm=nm: e.matmul(sap, lhsT=kap[0:krows, kc], rhs=qap[0:krows, i * 128:(i + 1) * 128],
                                                                               start=True, stop=(nm == 0)),
                         reads=kkeys + qkeys, writes=[("pb", sl)])
                    for mi, (ml, mr, mk) in enumerate(pem):
                        P.op("tensor", lambda e, sap=sap, ml=ml, mr=mr, mi=mi, nm=nm: e.matmul(sap, lhsT=ml, rhs=mr, start=False, stop=(mi == nm - 1)),
                             reads=mk, writes=[("pb", sl)])
                    for (_n, mr, mk) in dvm:
                        P.op("vector", lambda e, sap=sap, mr=mr: e.tensor_tensor(out=sap, in0=sap, in1=mr, op=ALU.add), reads=mk + [("pb", sl)], writes=[("pb", sl)])
                    bap, bkeys = bias_fn(i, vs)
                    P.op("scalar", lambda e, sap=sap, pl=pl, bap=bap: e.activation(out=pts[pl][:], in_=sap, func=ACT.Exp, bias=bap, scale=0.125),
                         reads=[("pb", sl)] + bkeys, writes=[("pt", pl)])
                m = n - LAG
                if m >= 0 and not DBG.get("nopv"):
                    i, kc, vs, masks, first, last, pl = flat[m]
                    if first:
                        oslot[i] = st_['o']; st_['o'] ^= 1
                    ob = oslot[i]
                    oap = PB[4 + ob][:, 0:ow]
                    P.op("tensor", lambda e, oap=oap, pl=pl, vs=vs, first=first, last=last: e.matmul(oap, lhsT=pts[pl][:], rhs=v_fn(vs), start=first, stop=last),
                         reads=[("pt", pl)] + vkeys, writes=[("pb", 4 + ob)])
                    if last:
                        finalize(i, PB[4 + ob], ("pb", 4 + ob))

        ptc = [P.sb(st, [128, 512], BF16, "ptc%d" % i) for i in range(4)]

        def run_head_chunked(kap, kkeys, qap, qkeys, v_fn, vkeys, ow, bias_col, tile_bias, finalize, extra=None, krows=67):
            steps = []
            for c2 in range(2):
                for kb in range(16 * c2 + 16):
                    steps.append([c2, kb, max(4 * c2, kb // 4)])
            LAG = DBG.get("clag", 2)
            for n in range(len(steps) + LAG):
                if n < len(steps):
                    c2, kb, i_min = steps[n]
                    s = slot_of(kb)
                    sl = (0, 1, 6)[st_['cs']]; st_['cs'] = (st_['cs'] + 1) % 3
                    pl = st_['cp']; st_['cp'] = (pl + 1) % 4
                    steps[n] += [sl, pl]
                    nt = 4 * c2 + 4 - i_min
                    ncols = nt * 128
                    q0 = i_min * 128
                    masks = []
                    if extra is not None:
                        ml, mr, mk = extra(s, q0, ncols)
                        masks.append((ml, mr, mk, 0, ncols))
                    if kb // 4 >= 4 * c2:
                        masks.append((ident[:], camb[:, kb - 4 * (kb // 4), :], ["ident", ("c", "camb")], 0, 128))
                    pem = [mm for mm in masks if mm[0] is not None]
                    dvm = [mm for mm in masks if mm[0] is None]
                    nm = len(pem)
                    P.op("tensor", lambda e, sl=sl, s=s, q0=q0, ncols=ncols, nm=nm: e.matmul(
                        PB[sl][:, 0:ncols], lhsT=kap[0:krows, s * 128:(s + 1) * 128], rhs=qap[0:krows, q0:q0 + ncols], start=True, stop=(nm == 0)),
                         reads=kkeys + qkeys, writes=[("pb", sl)])
                    for mi, (ml, mr, mk, c0, ncm) in enumerate(pem):
                        P.op("tensor", lambda e, sl=sl, ml=ml, mr=mr, c0=c0, ncm=ncm, mi=mi, nm=nm: e.matmul(
                            PB[sl][:, c0:c0 + ncm], lhsT=ml, rhs=mr, start=False, stop=(mi == nm - 1)), reads=mk, writes=[("pb", sl)])
                    for (_n, mr, mk, c0, ncm) in dvm:
                        P.op("vector", lambda e, sl=sl, mr=mr, c0=c0, ncm=ncm: e.tensor_tensor(out=PB[sl][:, c0:c0 + ncm], in0=PB[sl][:, c0:c0 + ncm], in1=mr, op=ALU.add),
                             reads=mk + [("pb", sl)], writes=[("pb", sl)])
                    for _d in range(DBG.get("dummy", 0)):
                        P.op("tensor", lambda e: e.matmul(PB[6][:, 0:512], lhsT=ident[:], rhs=cmask[:, 0, 0:512], start=True, stop=True),
                             reads=["ident", ("c", "cmask")], writes=[("pb", 6)])
                    if tile_bias is None:
                        bap, bkeys = bias_col(s)
                        P.op("scalar", lambda e, sl=sl, pl=pl, ncols=ncols, bap=bap: e.activation(out=ptc[pl][:, 0:ncols], in_=PB[sl][:, 0:ncols],
                                                                                                func=ACT.Exp, bias=bap, scale=0.125),
                             reads=[("pb", sl)] + bkeys, writes=[("ptc", pl)])
                    else:
                        for t in range(nt):
                            bap, bkeys = tile_bias(i_min + t, s)
                            P.op("scalar", lambda e, sl=sl, pl=pl, t=t, bap=bap: e.activation(out=ptc[pl][:, t * 128:(t + 1) * 128], in_=PB[sl][:, t * 128:(t + 1) * 128],
                                                                                             func=ACT.Exp, bias=bap, scale=0.125),
                                 reads=[("pb", sl)] + bkeys, writes=[("ptc", pl)])
                m = n - LAG
                if m >= 0:
                    c2, kb, i_min, sl, pl = steps[m]
                    s = slot_of(kb)
                    for t in range(4 * c2 + 4 - i_min):
                        i = i_min + t
                        ob = 2 + i % 4
                        first, last = (kb == 0), (kb == 4 * i + 3)
                        P.op("tensor", lambda e, ob=ob, pl=pl, t=t, s=s, first=first, last=last: e.matmul(
                            PB[ob][:, 0:ow], lhsT=ptc[pl][:, t * 128:(t + 1) * 128], rhs=v_fn(s), start=first, stop=last),
                             reads=[("ptc", pl)] + vkeys, writes=[("pb", ob)])
                        if last:
                            finalize(i, PB[ob], ("pb", ob))

        def causal_pairs(extra=None):
            out = []
            for i in range(8):
                lst = []
                for kb in range(4 * i + 4):
                    s = slot_of(kb)
                    masks = []
                    if extra is not None:
                        masks += extra(i, s)
                    if kb >= 4 * i:
                        masks.append((ident[:], camb[:, kb - 4 * i, :], ["ident", ("c", "camb")]))
                    lst.append((slice(s * 128, (s + 1) * 128), s, masks))
                out.append((i, lst))
            return out

        def band_pairs(nb, mtile, moff):
            out = []
            for i in range(8):
                lst = []
                for kb in range(max(0, 4 * i - nb), 4 * i + 4):
                    s = slot_of(kb)
                    lst.append((slice(s * 128, (s + 1) * 128), s, [(ident[:], mtile[:, moff + kb - 4 * i + nb, :], ["ident"] + CK)]))
                out.append((i, lst))
            return out

        def recip_den(den_ap, okey, clamp=False):
            r, rk = sm()
            if clamp:
                P.op("vector", lambda e: e.tensor_scalar(out=r, in0=den_ap, scalar1=1e-30, scalar2=None, op0=ALU.max), reads=[okey], writes=[rk])
                P.op("vector", lambda e: e.reciprocal(out=r, in_=r), reads=[rk], writes=[rk])
            else:
                P.op("vector", lambda e: e.reciprocal(out=r, in_=den_ap), reads=[okey], writes=[rk])
            return r, rk

        F = P.sb(st, [128, 32, 8], F32, "F")
        X = P.sb(st, [128, 7, 32], F32, "X")
        cum = P.sb(st, [128, 7, 32], F32, "cum")
        exs = P.sb(st, [128, 7, 32], F32, "exs")
        tots = P.sb(st, [128, 7, 32], F32, "tots")
        tri = P.sb(st, [128, 128], F32, "tri")
        onef = P.sb(st, [128, 128], F32, "onef")
        fbt = P.sb(st, [128, 7], F32, "fbt")
        P.op("gpsimd", lambda e: e.memset(onef[:], 1.0), writes=["onef"])
        P.op("gpsimd", lambda e: e.memset(tri[:], 1.0), writes=["tri"])
        P.op("gpsimd", lambda e: e.affine_select(out=tri[:], in_=tri[:], pattern=[[1, 128]], compare_op=ALU.is_ge, fill=0.0,
                                                  base=0, channel_multiplier=-1), reads=["tri"], writes=["tri"])
        P.dma("sync", lambda e: e.dma_start(out=fbt[:], in_=foxb), writes=["fbt"])
        for r in range(4):
            P.dma("sync", lambda e, r=r: e.dma_start(out=F[:, r * 8:(r + 1) * 8, :], in_=fg[r][:, :, 0:8]), writes=[("F", r)], semkey="Fsem")
        FK = [("F", r) for r in range(4)]
        for h in range(7):
            P.op("vector", lambda e, h=h: e.tensor_scalar(out=X[:, h, :], in0=F[:, :, h], scalar1=fbt[:, h:h + 1], scalar2=None, op0=ALU.add),
                 reads=FK + ["fbt"], writes=["X"])
        P.op("scalar", lambda e: e.activation(out=X[:], in_=X[:], func=ACT.Exp, scale=-1.0), reads=["X"], writes=["X"])
        P.op("vector", lambda e: e.tensor_scalar(out=X[:], in0=X[:], scalar1=1.0, scalar2=None, op0=ALU.add), reads=["X"], writes=["X"])
        P.op("scalar", lambda e: e.activation(out=X[:], in_=X[:], func=ACT.Ln), reads=["X"], writes=["X"])
        Xf = X[:].rearrange("p h s -> p (h s)")
        P.op("tensor", lambda e: e.matmul(PB[6][:, 0:224], lhsT=tri[:], rhs=Xf, start=True, stop=True), reads=["X", "tri"], writes=[("pb", 6)])
        P.op("tensor", lambda e: e.matmul(PB[5][:, 0:224], lhsT=onef[:], rhs=Xf, start=True, stop=True), reads=["X", "onef"], writes=[("pb", 5)])
        P.op("vector", lambda e: e.tensor_copy(out=tots[:].rearrange("p h s -> p (h s)"), in_=PB[5][:, 0:224]), reads=[("pb", 5)], writes=["tots"])
        P.op("vector", lambda e: e.memset(exs[:, :, 0:1], 0.0), writes=["exs"])
        for kb in range(1, 32):
            s1, s0 = slot_of(kb), slot_of(kb - 1)
            P.op("vector", lambda e, s1=s1, s0=s0: e.tensor_tensor(out=exs[:, :, s1], in0=exs[:, :, s0], in1=tots[:, :, s0], op=ALU.add),
                 reads=["exs", "tots"], writes=["exs"])
        P.op("vector", lambda e: e.tensor_tensor(out=cum[:].rearrange("p h s -> p (h s)"), in0=PB[6][:, 0:224],
                                                 in1=exs[:].rearrange("p h s -> p (h s)"), op=ALU.add),
             reads=[("pb", 6), "exs"], writes=["cum"])
        refq = P.sb(st, [128, 7, 8], F32, "refq")
        mid = P.sb(st, [128, 7, 32], F32, "mid")
        P.op("vector", lambda e: e.scalar_tensor_tensor(out=mid[:], in0=tots[:], scalar=0.5, in1=exs[:], op0=ALU.mult, op1=ALU.add),
             reads=["tots", "exs"], writes=["mid"])
        P.op("vector", lambda e: e.tensor_scalar(out=refq[:], in0=mid[:, :, 0:8], scalar1=onehot[:, 0:1], scalar2=None, op0=ALU.mult),
             reads=["mid"] + CK, writes=["refq"])
        for dl in range(1, 4):
            P.op("vector", lambda e, dl=dl: e.scalar_tensor_tensor(out=refq[:], in0=mid[:, :, dl * 8:(dl + 1) * 8], scalar=onehot[:, dl:dl + 1], in1=refq[:],
                                                                    op0=ALU.mult, op1=ALU.add), reads=["mid", "refq"] + CK, writes=["refq"])
        Rv = P.sb(st, [67, 8], F32, "Rv")
        Rr = P.sb(st, [67, 8], F32, "Rr")
        Rh = [P.sb(st, [67, 8], BF16, "Rh%d" % k) for k in range(3)]
        Rt = P.sb(st, [67, 8], F32, "Rt")
        for h in range(min(7, DBG.get("nhead", 7)) if _on('fox') else 0):
            kj = load_k(KH_FOX + h)
            j = st_['q']; st_['q'] ^= 1
            qj = j
            P.dma("sync", lambda e, j=j, h=h: e.dma_start(out=Qb[j][0:64, :], in_=qT_in[QH_FOX + h]), writes=[("Qb", j)], semkey=("Qbsem", j))
            P.op("vector", lambda e, h=h: e.tensor_scalar(out=Rv[64:67, :], in0=refq[64:67, h, :], scalar1=-8.0, scalar2=None, op0=ALU.mult), reads=["refq"], writes=["Rv"])
            P.op("vector", lambda e: e.tensor_copy(out=Rh[0][64:67, :], in_=Rv[64:67, :]), reads=["Rv"], writes=["Rh0"])
            P.op("vector", lambda e: e.tensor_tensor(out=Rr[64:67, :], in0=Rv[64:67, :], in1=Rh[0][64:67, :], op=ALU.subtract), reads=["Rv", "Rh0"], writes=["Rr"])
            P.op("vector", lambda e: e.tensor_copy(out=Rh[1][64:67, :], in_=Rr[64:67, :]), reads=["Rr"], writes=["Rh1"])
            P.op("vector", lambda e: e.tensor_tensor(out=Rr[64:67, :], in0=Rr[64:67, :], in1=Rh[1][64:67, :], op=ALU.subtract), reads=["Rr", "Rh1"], writes=["Rr"])
            P.op("vector", lambda e: e.tensor_copy(out=Rh[2][64:67, :], in_=Rr[64:67, :]), reads=["Rr"], writes=["Rh2"])
            P.op("vector", lambda e: e.tensor_scalar(out=Rt[64:67, :], in0=Rh[0][64:67, :], scalar1=sel3[64:67, 0:1], scalar2=None, op0=ALU.mult),
                 reads=["Rh0"] + CK, writes=["Rt"])
            P.op("vector", lambda e: e.scalar_tensor_tensor(out=Rt[64:67, :], in0=Rh[1][64:67, :], scalar=sel3[64:67, 1:2], in1=Rt[64:67, :], op0=ALU.mult, op1=ALU.add),
                 reads=["Rh1", "Rt"] + CK, writes=["Rt"])
            P.op("vector", lambda e: e.scalar_tensor_tensor(out=Rt[64:67, :], in0=Rh[2][64:67, :], scalar=sel3[64:67, 2:3], in1=Rt[64:67, :], op0=ALU.mult, op1=ALU.add),
                 reads=["Rh2", "Rt"] + CK, writes=["Rt"])
            for i in range(8):
                P.op("vector", lambda e, j=j, i=i: e.tensor_scalar(out=Qb[j][64:67, i * 128:(i + 1) * 128], in0=onef[64:67, :], scalar1=Rt[64:67, i:i + 1], scalar2=None,
                                                                    op0=ALU.mult), reads=["Rt", "onef"], writes=[("Qx", j)])
            vj = load_v(VH_FOX + h, 64)

            def fin(i, ob, okey, h=h):
                r, rk = recip_den(ob[:, 64:65], okey)
                P.op("vector", lambda e: e.tensor_scalar(out=otm[:, i, h * 64:(h + 1) * 64], in0=ob[:, 0:64], scalar1=r, scalar2=None, op0=ALU.mult),
                     reads=[okey, rk], writes=[("otm", i)])

            run_head_chunked(Kb[kj], [("Kb", kj), ("Kb1", kj)], Qb[qj], [("Qb", qj), ("Qx", qj)],
                             lambda vs, vj=vj: V64[vj][:, vs, 0:65], [("V", 64, vj, r) for r in range(4)] + [("V1", 64, vj)], 65,
                             lambda vs, h=h: (cum[:, h, vs:vs + 1], ["cum"]), None, fin)

        lamt = P.sb(st, [128, 4, 64], F32, "lamt")
        lam = P.sb(st, [128, 4], F32, "lam")
        subg = P.sb(st, [128, 128], F32, "subg")
        lit = P.sb(st, [128, 2], F32, "lit")
        dt0 = P.sb(st, [128, 128], F32, "dt0")
        dt1 = P.sb(st, [128, 128], F32, "dt1")
        d0acc = P.sb(st, [128, 8, 128], F32, "d0acc")
        P.dma("sync", lambda e: e.dma_start(out=lamt[:], in_=dlam), writes=["lamt"])
        P.dma("sync", lambda e: e.dma_start(out=subg[:], in_=dsub), writes=["subg"])
        P.dma("sync", lambda e: e.dma_start(out=lit[:], in_=lami), writes=["lit"])
        P.op("vector", lambda e: e.tensor_tensor(out=lamt[:, 0, :], in0=lamt[:, 0, :], in1=lamt[:, 1, :], op=ALU.mult), reads=["lamt"], writes=["lamt"])
        P.op("vector", lambda e: e.tensor_tensor(out=lamt[:, 2, :], in0=lamt[:, 2, :], in1=lamt[:, 3, :], op=ALU.mult), reads=["lamt"], writes=["lamt"])
        P.op("vector", lambda e: e.tensor_reduce(out=lam[:, 0:1], in_=lamt[:, 0, :], axis=AX.X, op=ALU.add), reads=["lamt"], writes=["lam"])
        P.op("vector", lambda e: e.tensor_reduce(out=lam[:, 1:2], in_=lamt[:, 2, :], axis=AX.X, op=ALU.add), reads=["lamt"], writes=["lam"])
        P.op("scalar", lambda e: e.activation(out=lam[:, 0:2], in_=lam[:, 0:2], func=ACT.Exp), reads=["lam"], writes=["lam"])
        P.op("vector", lambda e: e.tensor_tensor(out=lam[:, 2:3], in0=lam[:, 0:1], in1=lam[:, 1:2], op=ALU.subtract), reads=["lam"], writes=["lam"])
        P.op("vector", lambda e: e.tensor_tensor(out=lam[:, 2:3], in0=lam[:, 2:3], in1=lit[:, 0:1], op=ALU.add), reads=["lam", "lit"], writes=["lam"])
        P.op("vector", lambda e: e.tensor_scalar(out=lam[:, 3:4], in0=lam[:, 2:3], scalar1=-1.0, scalar2=None, op0=ALU.mult), reads=["lam"], writes=["lam"])
        for h in range(4 if _on('diff') else 0):
            vj = load_v(h, 128)
            vkeys = [("V", 128, vj, r) for r in range(4)] + [("V1", 128, 0)]
            for m in range(2):
                kj = load_k(KH_DIFF + 2 * h + m)
                qj = load_q(QH_DIFF + 2 * h + m, AX_DIFF + h)

                def fin(i, ob, okey, h=h, m=m):
                    r, rk = recip_den(ob[:, 128:129], okey)
                    if m == 0:
                        P.op("vector", lambda e: e.tensor_scalar(out=d0acc[:, i, :], in0=ob[:, 0:128], scalar1=r,
                                                                 scalar2=None, op0=ALU.mult), reads=[okey, rk], writes=[("d0acc", i)])
                    else:
                        f2, fk2 = sm()
                        P.op("vector", lambda e: e.tensor_tensor(out=f2, in0=r, in1=lam[:, 3:4], op=ALU.mult), reads=[rk, "lam"], writes=[fk2])
                        P.op("vector", lambda e: e.scalar_tensor_tensor(out=dt0[:], in0=ob[:, 0:128], scalar=f2,
                                                                        in1=d0acc[:, i, :], op0=ALU.mult, op1=ALU.add),
                             reads=[okey, fk2, ("d0acc", i)], writes=["dt0"])
                        ss, sk = sm()
                        P.op("vector", lambda e: e.tensor_tensor(out=dt1[:], in0=dt0[:], in1=dt0[:], op=ALU.mult), reads=["dt0"], writes=["dt1"])
                        P.op("vector", lambda e: e.tensor_reduce(out=ss, in_=dt1[:], axis=AX.X, op=ALU.add), reads=["dt1"], writes=[sk])
                        P.op("vector", lambda e: e.tensor_scalar(out=ss, in0=ss, scalar1=1.0 / 128, scalar2=EPS, op0=ALU.mult, op1=ALU.add), reads=[sk], writes=[sk])
                        P.op("scalar", lambda e: e.activation(out=ss, in_=ss, func=ACT.Ln), reads=[sk], writes=[sk])
                        P.op("scalar", lambda e: e.activation(out=ss, in_=ss, func=ACT.Exp, scale=-0.5), reads=[sk], writes=[sk])
                        P.op("vector", lambda e: e.tensor_scalar(out=dt0[:], in0=dt0[:], scalar1=ss, scalar2=lit[:, 1:2], op0=ALU.mult, op1=ALU.mult),
                             reads=["dt0", sk, "lit"], writes=["dt0"])
                        P.op("vector", lambda e: e.tensor_tensor(out=otm[:, i, 1536 + h * 128:1536 + (h + 1) * 128], in0=dt0[:], in1=subg[:], op=ALU.mult),
                             reads=["dt0", "subg"], writes=[("otm", i)])

                ax = AX_DIFF + h
                run_head_chunked(Kb[kj], [("Kb", kj), ("Kb1", kj)], Qb[qj], [("Qb", qj), ("Qx", qj)],
                                 lambda vs, vj=vj: V128[vj][:, vs, 0:129], vkeys, 129,
                                 lambda vs, ax=ax: (kbias[:, ax, vs:vs + 1], [("c", "kbias")]), None, fin)

        w1t = P.sb(st, [64, 2, 32, 64], BF16, "w1t")
        w2t = P.sb(st, [64, 2, 64], BF16, "w2t")
        post = P.sb(st, [64, 2, 32], BF16, "post")
        pbias = P.sb(st, [64, 2], F32, "pbias")
        Kc = P.sb(st, [67, 256], BF16, "Kc")
        VM = P.sb(st, [128, 2, 136], BF16, "VM")
        hx = P.sb(st, [64, 256], F32, "hx")
        hy = P.sb(st, [64, 256], F32, "hy")
        Gt = P.sb(st, [64, 256], BF16, "Gt")
        nacc = P.sb(st, [128, 8, 4, 64], F32, "nacc")
        imp = P.sb(st, [128, 8, 64], F32, "imp")
        sc1 = P.sb(st, [128, 64], F32, "sc1")
        sc2 = P.sb(st, [128, 64], F32, "sc2")
        m8 = P.sb(st, [128, 16], F32, "m8")
        nsel = P.sb(st, [128, 64], BF16, "nsel")
        nselT = P.sb(st, [64, 1024], BF16, "nselT")
        for kv in range(2):
            P.dma("gpsimd", lambda e, kv=kv: e.dma_start(out=w1t[:, kv], in_=cw1[kv]), writes=[("w1t", kv)], semkey=("cwsem", 1, kv))
            P.dma("gpsimd", lambda e, kv=kv: e.dma_start(out=w2t[:, kv], in_=cw2[kv]), writes=[("w2t", kv)], semkey=("cwsem", 2, kv))
            P.dma("gpsimd", lambda e, kv=kv: e.dma_start(out=post[:, kv], in_=cposT[kv]), writes=[("post", kv)], semkey=("cwsem", 3, kv))
        P.dma("sync", lambda e: e.dma_start(out=VM[:, :, 65:129], in_=cd['mcs']), writes=["VMm"])
        P.op("gpsimd", lambda e: e.memset(VM[:, :, 64:65], 1.0), writes=["VM1"])
        P.op("gpsimd", lambda e: e.memset(Kc[64:67, :], 1.0), writes=["Kc1"])
        for kv in range(2):
            for l in range(32):
                P.op("tensor", lambda e, kv=kv, l=l: e.matmul(PB[6][0:64, kv:kv + 1], lhsT=w1t[:, kv, l, :], rhs=post[:, kv, l:l + 1], start=(l == 0), stop=(l == 31)),
                     reads=[("w1t", kv), ("post", kv)], writes=[("pb", 6)])
        P.op("vector", lambda e: e.tensor_copy(out=pbias[:], in_=PB[6][0:64, 0:2]), reads=[("pb", 6)], writes=["pbias"])

        def nsa_compress(g, kv):
            kj = load_k((KH_CMPK if kv == 0 else KH_CMPV) + g)
            kk = [("Kb", kj)]
            H = PB[6][0:64, 0:256]
            H2 = PB[6][0:64, 256:512]
            Hc = PB[5][0:64, 0:32]
            for l in range(16):
                P.op("tensor", lambda e, l=l, kv=kv, kj=kj, H=H: e.matmul(H, lhsT=w1t[:, kv, l, :], rhs=Kb[kj][0:64, l:4096:16], start=(l == 0), stop=(l == 15)),
                     reads=kk + [("w1t", kv)], writes=[("pb", 6)])
            for l in range(16):
                P.op("tensor", lambda e, l=l, kv=kv, kj=kj, H2=H2: e.matmul(H2[:, 0:255], lhsT=w1t[:, kv, 16 + l, :], rhs=Kb[kj][0:64, 16 + l:4096:16],
                                                                        start=(l == 0), stop=(l == 15)),
                     reads=kk + [("w1t", kv)], writes=[("pb", 6)])
            for l in range(16):
                P.op("tensor", lambda e, l=l, kv=kv, kj=kj, Hc=Hc: e.matmul(Hc[:, 0:24], lhsT=w1t[:, kv, 16 + l, :], rhs=Kb[kj][0:64, 1024 + l:4096:128],
                                                                        start=(l == 0), stop=(l == 15)),
                     reads=kk + [("w1t", kv)], writes=[("pb", 5)])
            for l in range(16):
                P.op("tensor", lambda e, l=l, kv=kv, kj=kj, Hc=Hc: e.matmul(Hc[:, 24:31], lhsT=w1t[:, kv, 16 + l, :], rhs=Kb[kj][0:64, 128 + l:1024:128],
                                                                        start=(l == 0), stop=(l == 15)),
                     reads=kk + [("w1t", kv)], writes=[("pb", 5)])
            hx3 = hx[:].rearrange("d (s c) -> d s c", c=8)
            H23 = H2.rearrange("d (s c) -> d s c", c=8)
            P.op("vector", lambda e, kv=kv, H=H: e.tensor_scalar(out=hx[:], in0=H, scalar1=pbias[:, kv:kv + 1], scalar2=None, op0=ALU.add), reads=[("pb", 6), "pbias"], writes=["hx"])
            P.op("vector", lambda e, hx3=hx3, H23=H23: e.tensor_tensor(out=hx3[:, :, 0:7], in0=hx3[:, :, 0:7], in1=H23[:, :, 0:7], op=ALU.add), reads=[("pb", 6), "hx"], writes=["hx"])
            P.op("vector", lambda e, hx3=hx3, Hc=Hc: e.tensor_tensor(out=hx3[:, 0:31, 7], in0=hx3[:, 0:31, 7], in1=Hc[:, 0:31], op=ALU.add), reads=[("pb", 5), "hx"], writes=["hx"])
            P.op("vector", lambda e: e.tensor_tensor(out=hy[:], in0=hx[:], in1=hx[:], op=ALU.mult), reads=["hx"], writes=["hy"])
            P.op("vector", lambda e: e.tensor_scalar(out=hy[:], in0=hy[:], scalar1=0.044715, scalar2=1.0, op0=ALU.mult, op1=ALU.add), reads=["hy"], writes=["hy"])
            P.op("vector", lambda e: e.tensor_tensor(out=hy[:], in0=hy[:], in1=hx[:], op=ALU.mult), reads=["hy", "hx"], writes=["hy"])
            P.op("scalar", lambda e: e.activation(out=hy[:], in_=hy[:], func=ACT.Sigmoid, scale=1.5957691216057308), reads=["hy"], writes=["hy"])
            P.op("vector", lambda e: e.tensor_tensor(out=Gt[:], in0=hy[:], in1=hx[:], op=ALU.mult), reads=["hy", "hx"], writes=["Gt"])
            if kv == 0:
                P.op("tensor", lambda e: e.matmul(PB[5][0:64, 0:256], lhsT=w2t[:, 0, :], rhs=Gt[:], start=True, stop=True), reads=["Gt", ("w2t", 0)], writes=[("pb", 5)])
                P.op("vector", lambda e: e.tensor_copy(out=Kc[0:64, :], in_=PB[5][0:64, 0:256]), reads=[("pb", 5)], writes=["Kc"])
            else:
                for blk in range(2):
                    P.op("tensor", lambda e, blk=blk: e.matmul(PB[5][:, 0:64], lhsT=Gt[:, blk * 128:(blk + 1) * 128], rhs=w2t[:, 1, :], start=True, stop=True),
                         reads=["Gt", ("w2t", 1)], writes=[("pb", 5)])
                    P.op("vector", lambda e, blk=blk: e.tensor_copy(out=VM[:, blk, 0:64], in_=PB[5][:, 0:64]), reads=[("pb", 5)], writes=[("VM", blk)])

        def nsa_cmp_head(g, j):
            hh = 4 * g + j
            qj = load_q(QH_NSA + hh, AX_NSA + hh)
            pairs = [(i, [(slice(blk * 128, (blk + 1) * 128), blk, [(ident[:], cmask[:, blk, i * 128:(i + 1) * 128], ["ident"] + CK)]) for blk in range(2)])
                     for i in range(8)]

            def fin(i, ob, okey, j=j, hh=hh):
                r, rk = recip_den(ob[:, 64:65], okey, clamp=True)
                f2, fk2 = sm()
                P.op("vector", lambda e: e.tensor_tensor(out=f2, in0=r, in1=gate[:, i, hh * 3:hh * 3 + 1], op=ALU.mult), reads=[rk, "gate"], writes=[fk2])
                P.op("vector", lambda e: e.tensor_scalar(out=nacc[:, i, j, :], in0=ob[:, 0:64], scalar1=f2, scalar2=None, op0=ALU.mult),
                     reads=[okey, fk2], writes=[("nacc", i, j)])
                if j == 0:
                    P.op("vector", lambda e: e.tensor_scalar(out=imp[:, i, :], in0=ob[:, 65:129], scalar1=r, scalar2=None, op0=ALU.mult),
                         reads=[okey, rk], writes=[("imp", i)])
                else:
                    P.op("vector", lambda e: e.scalar_tensor_tensor(out=imp[:, i, :], in0=ob[:, 65:129], scalar=r, in1=imp[:, i, :], op0=ALU.mult, op1=ALU.add),
                         reads=[okey, rk, ("imp", i)], writes=[("imp", i)])

            run_head(Kc, ["Kc", "Kc1"], Qb[qj], [("Qb", qj), ("Qx", qj)],
                     lambda vs: VM[:, vs, 0:129], [("VM", 0), ("VM", 1), "VMm", "VM1"], 129,
                     pairs, lambda i, vs, hh=hh: (kbias_cmp[:, hh, vs:vs + 1], CK), fin)

        def nsa_select(g):
            for i in range(8):
                P.op("vector", lambda e, i=i: e.tensor_tensor(out=sc1[:], in0=imp[:, i, :], in1=atab[:, i, :], op=ALU.mult), reads=[("imp", i)] + CK, writes=["sc1"])
                P.op("vector", lambda e, i=i: e.tensor_tensor(out=sc1[:], in0=sc1[:], in1=btab[:, i, :], op=ALU.add), reads=["sc1"] + CK, writes=["sc1"])
                P.op("vector", lambda e: e.max(out=m8[:, 0:8], in_=sc1[:]), reads=["sc1"], writes=["m8"])
                P.op("vector", lambda e: e.match_replace(out=sc2[:], in_to_replace=m8[:, 0:8], in_values=sc1[:], imm_value=-3.0e38), reads=["sc1", "m8"], writes=["sc2"])
                P.op("vector", lambda e: e.max(out=m8[:, 8:16], in_=sc2[:]), reads=["sc2"], writes=["m8"])
                thr, tk_ = sm()
                P.op("vector", lambda e, thr=thr: e.tensor_reduce(out=thr, in_=m8[:, 8:16], axis=AX.X, op=ALU.min), reads=["m8"], writes=[tk_])
                P.op("vector", lambda e, thr=thr: e.tensor_scalar(out=sc2[:], in0=sc1[:], scalar1=thr, scalar2=None, op0=ALU.is_ge), reads=["sc1", tk_], writes=["sc2"])
                P.op("vector", lambda e: e.tensor_scalar(out=nsel[:], in0=sc2[:], scalar1=-1.0, scalar2=-NEG, op0=ALU.add, op1=ALU.mult), reads=["sc2"], writes=["nsel"])
                P.op("tensor", lambda e: e.transpose(self.PT16[0:64, 0:128], nsel[:], ident[:]), reads=["nsel", "ident"], writes=["pt16"])
                P.op("vector", lambda e, i=i: e.tensor_copy(out=nselT[:, i * 128:(i + 1) * 128], in_=self.PT16[0:64, 0:128]), reads=["pt16"], writes=[("nselT", i)])

        def nsa_heads(g):
            ksl = load_k(KH_SLC + g)
            vsl = load_v(VH_SLC + g, 64)
            kwn = load_k(KH_WIN + g)
            vwn = load_v(VH_WIN + g, 64)
            for j in range(4):
                hh = 4 * g + j
                qj = load_q(QH_NSA + hh, AX_NSA + hh)
                ax = AX_NSA + hh

                def fin_s(i, ob, okey, j=j, hh=hh):
                    r, rk = recip_den(ob[:, 64:65], okey)
                    f2, fk2 = sm()
                    P.op("vector", lambda e: e.tensor_tensor(out=f2, in0=r, in1=gate[:, i, hh * 3 + 1:hh * 3 + 2], op=ALU.mult), reads=[rk, "gate"], writes=[fk2])
                    P.op("vector", lambda e: e.scalar_tensor_tensor(out=nacc[:, i, j, :], in0=ob[:, 0:64], scalar=f2, in1=nacc[:, i, j, :], op0=ALU.mult, op1=ALU.add),
                         reads=[okey, fk2, ("nacc", i, j)], writes=[("nacc", i, j)])

                def fin_w(i, ob, okey, j=j, hh=hh):
                    r, rk = recip_den(ob[:, 64:65], okey)
                    f2, fk2 = sm()
                    P.op("vector", lambda e: e.tensor_tensor(out=f2, in0=r, in1=gate[:, i, hh * 3 + 2:hh * 3 + 3], op=ALU.mult), reads=[rk, "gate"], writes=[fk2])
                    P.op("vector", lambda e: e.scalar_tensor_tensor(out=otm[:, i, 448 + hh * 64:448 + (hh + 1) * 64], in0=ob[:, 0:64], scalar=f2, in1=nacc[:, i, j, :],
                                                                    op0=ALU.mult, op1=ALU.add),
                         reads=[okey, fk2, ("nacc", i, j)], writes=[("otm", i)])

                run_head_chunked(Kb[ksl], [("Kb", ksl), ("Kb1", ksl)], Qb[qj], [("Qb", qj), ("Qx", qj)],
                                 lambda vs, vsl=vsl: V64[vsl][:, vs, 0:65], [("V", 64, vsl, r) for r in range(4)] + [("V1", 64, vsl)], 65,
                                 lambda vs, ax=ax: (kbias[:, ax, vs:vs + 1], CK), None, fin_s,
                                 extra=lambda s, q0, ncols: (esel[:, s, :], nselT[:, q0:q0 + ncols], [("nselT", i) for i in range(8)] + CK))
                run_head(Kb[kwn], [("Kb", kwn), ("Kb1", kwn)], Qb[qj], [("Qb", qj), ("Qx", qj)],
                         lambda vs, vwn=vwn: V64[vwn][:, vs, 0:65], [("V", 64, vwn, r) for r in range(4)] + [("V1", 64, vwn)], 65,
                         band_pairs(4, wmask, 0), lambda i, vs, ax=ax: (kbias[:, ax, vs:vs + 1], CK), fin_w)


        nsa_on = _on('nsa')
        pieces0 = ([lambda: nsa_compress(0, 0), lambda: nsa_compress(0, 1)] + [lambda j=j: nsa_cmp_head(0, j) for j in range(4)]
                   + [lambda: nsa_select(0)]) if nsa_on else []

        dnum = P.sb(st, [128, 8, 9, 65], F32, "dnum")
        for h in range(9 if _on('dil') else 0):
            g = h // 3
            win, dil, nb = DIL_CFG[g]
            kj = load_k(KH_DIL + h)
            qj = load_q(QH_DIL + h, AX_DIL + h)
            vj = load_v(VH_DIL + h, 64)

            def fin(i, ob, okey, h=h):
                P.op("vector", lambda e: e.tensor_copy(out=dnum[:, i, h, :], in_=ob[:, 0:65]), reads=[okey], writes=[("dnum", i)])

            ax = AX_DIL + h
            run_head(Kb[kj], [("Kb", kj), ("Kb1", kj)], Qb[qj], [("Qb", qj), ("Qx", qj)],
                     lambda vs, vj=vj: V64[vj][:, vs, 0:65], [("V", 64, vj, r) for r in range(4)] + [("V1", 64, vj)], 65,
                     band_pairs(nb, dmask, DIL_MOFF[g]), lambda i, vs, ax=ax: (kbias[:, ax, vs:vs + 1], [("c", "kbias")]), fin)
            if pieces0:
                pieces0.pop(0)()
        for i in range(8 if _on('dil') else 0):
            for j in range(3):
                r, rk = sm()
                P.op("vector", lambda e, i=i, j=j, r=r: e.tensor_tensor(out=r, in0=dnum[:, i, j, 64:65], in1=dnum[:, i, 3 + j, 64:65], op=ALU.add), reads=[("dnum", i)], writes=[rk])
                P.op("vector", lambda e, i=i, j=j, r=r: e.tensor_tensor(out=r, in0=r, in1=dnum[:, i, 6 + j, 64:65], op=ALU.add), reads=[("dnum", i), rk], writes=[rk])
                P.op("vector", lambda e, r=r: e.reciprocal(out=r, in_=r), reads=[rk], writes=[rk])
                for g in range(3):
                    h = 3 * g + j
                    P.op("vector", lambda e, i=i, h=h, r=r: e.tensor_scalar(out=otm[:, i, 960 + h * 64:960 + (h + 1) * 64], in0=dnum[:, i, h, 0:64], scalar1=r,
                                                                            scalar2=None, op0=ALU.mult), reads=[("dnum", i), rk], writes=[("otm", i)])

        while pieces0:
            pieces0.pop(0)()
        if nsa_on:
            nsa_heads(0)
            nsa_compress(1, 0)
            nsa_compress(1, 1)
            for j in range(4):
                nsa_cmp_head(1, j)
            nsa_select(1)
            nsa_heads(1)

        if dbg_o is not None:
            P.dma("sync", lambda e: e.dma_start(out=dbg_o, in_=otm[:]), reads=[("otm", i) for i in range(8)], writes=["dbg"])
        uT = self.uT
        for i in range(8):
            for c4 in range(4):
                for cc in range(4):
                    c = 4 * c4 + cc
                    P.op("tensor", lambda e, i=i, c=c, cc=cc: e.transpose(self.PT16[:, cc * 128:(cc + 1) * 128], otm[:, i, c * 128:(c + 1) * 128], ident[:]),
                         reads=[("otm", i), "ident"], writes=["pt16"])
                eng = self.evac_engine()
                for cc in range(4):
                    c = 4 * c4 + cc
                    self.copy(eng, uT[:, c, i * 128:(i + 1) * 128], self.PT16[:, cc * 128:(cc + 1) * 128], ["pt16"], [("uT", c, i // 4)])
        P.barrier()
        P.emit()


Builder.attention = _attention


_PROGS = {}


def _prog(has_b, has_a, final, debug=False):
    key = (has_b, has_a, final, debug)
    if key not in _PROGS:
        _PROGS[key] = Builder(has_b, has_a, final, debug).build()
    return _PROGS[key]


def _pl(v):
    return np.ascontiguousarray(np.asarray(v, np.float32).reshape(-1, 128).T)


def _rep(v):
    v = np.asarray(v, np.float32)
    return np.ascontiguousarray(np.broadcast_to(v[None], (128,) + v.shape))


def kernel(x, c, ada_w, ada_b, norm_mix_g, norm_ffn_g, w_in, fox_f_bias, nsa_cmp_w1, nsa_cmp_w2, nsa_cmp_pos,
           diff_lambda, diff_subln_g, w_out, ffn_w_gate, ffn_w_up, ffn_w_down, final_norm_g, _debug=None):
    x = np.asarray(x, np.float32)
    consts = [make_consts(r) for r in range(4)]
    cores = list(range(8))
    base = []
    h = []
    for core in cores:
        b, r = core // 4, core % 4
        m = {"c_" + k: v for k, v in consts[r].items()}
        m["cT"] = _pl(c[b])
        base.append(m)
        xl = x[b].reshape(8, 4, 128, D)[:, r].reshape(1024, D)
        h.append(np.ascontiguousarray(xl.T.reshape(16, 128, 1024).transpose(1, 0, 2)))

    def a_inputs(l):
        wl = np.asarray(w_in[l], np.float32)
        w_tm = np.zeros((D, 1824), np.float32)
        w_tm[:, :1823] = wl[:, TM_COLS]
        d = dict(gmix=_pl(norm_mix_g[l]), w_fm=np.ascontiguousarray(wl[:, FM_COLS]), w_tm=w_tm)
        if l == 0:
            d.update(ada_w=np.ascontiguousarray(ada_w[0], np.float32), ada_b=_pl(ada_b[0]))
        return d

    cT2 = np.ascontiguousarray(np.stack([_pl(c[0]), _pl(c[1])], axis=-1))
    modL = {}

    def b_inputs(l):
        lam_init = 0.8 - 0.6 * math.exp(-0.3 * l)
        w1 = np.asarray(nsa_cmp_w1[l], np.float32).reshape(2, 32, 64, 64).transpose(0, 2, 1, 3)
        return dict(w_out=np.ascontiguousarray(w_out[l], np.float32), w_gate=np.ascontiguousarray(ffn_w_gate[l], np.float32),
                    w_up=np.ascontiguousarray(ffn_w_up[l], np.float32), w_down=np.ascontiguousarray(ffn_w_down[l], np.float32),
                    gffn=_pl(norm_ffn_g[l]), foxb=_rep(fox_f_bias[l]), cw1=np.ascontiguousarray(w1),
                    cw2=np.ascontiguousarray(nsa_cmp_w2[l], np.float32),
                    cposT=np.ascontiguousarray(np.asarray(nsa_cmp_pos[l], np.float32).transpose(0, 2, 1)),
                    dlam=_rep(diff_lambda[l]), dsub=_rep(diff_subln_g[l]),
                    lami=_rep(np.array([lam_init, 1.0 - lam_init], np.float32)))

    prev = None
    for stage in range(DEPTH + 1):
        has_b = stage > 0
        has_a = stage < DEPTH
        final = stage == DEPTH
        dbg = (_debug is not None and _debug == stage)
        nc = _prog(has_b, has_a, final, dbg)
        ai = a_inputs(stage) if has_a else {}
        bi = b_inputs(stage - 1) if has_b else {}
        maps = []
        for core in cores:
            m = dict(base[core])
            m["h_in"] = h[core]
            m.update(ai)
            m.update(bi)
            if has_a and not has_b:
                m["cT2"] = cT2
                m["ada_wx"] = np.ascontiguousarray(np.asarray(ada_w[1:4], np.float32)[:, :, core * 1536:(core + 1) * 1536])
                m["ada_bx"] = np.ascontiguousarray(np.asarray(ada_b[1:4], np.float32)[:, core * 1536:(core + 1) * 1536]
                                                   .reshape(3, 12, 128).transpose(2, 0, 1))
            if has_a and has_b:
                m["modA"] = modL[(stage, core // 4)]
            if has_b:
                g0 = (core // 4) * 4
                m["kTg"] = np.stack([prev[g0 + r]["kT_o"] for r in range(4)])
                m["v64g"] = np.stack([prev[g0 + r]["v64_o"] for r in range(4)])
                m["v128g"] = np.stack([prev[g0 + r]["v128_o"] for r in range(4)])
                m["fg"] = np.stack([prev[g0 + r]["f_o"] for r in range(4)])
                m["fown"] = prev[core]["f_o"]
                m["qT_in"] = prev[core]["qT_o"]
                m["modB"] = prev[core]["mod_o"] if stage == 1 else modL[(stage - 1, core // 4)]
            if final:
                m["gfin"] = _pl(final_norm_g)
            maps.append(m)
        res = run_bass_kernel_spmd(nc, maps, core_ids=cores).results
        if stage > 0:
            h = [np.asarray(res[core]["h_out"], np.float32) for core in cores]
        if stage == 0:
            mx = np.stack([np.asarray(res[core]["modx_o"], np.float32) for core in cores])
            for l in range(1, 4):
                for b in range(2):
                    modL[(l, b)] = np.ascontiguousarray(mx[:, :, l - 1, :, b].transpose(1, 0, 2).reshape(128, 96))
        prev = res
        if dbg:
            return res
    out = np.zeros((2, 32, 128, D), np.float32)
    for core in cores:
        b, r = core // 4, core % 4
        yl = h[core].transpose(1, 0, 2).reshape(D, 1024).T
        out[b, r::4] = yl.reshape(8, 128, D)
    return out.reshape(2, S, D)
```

```python
import math
from contextlib import ExitStack

import numpy as np
import ml_dtypes
import concourse.bass as bass
import concourse.mybir as mybir
from concourse.bass_utils import run_bass_kernel_spmd

F32 = mybir.dt.float32
BF16 = mybir.dt.bfloat16
ALU = mybir.AluOpType
ACT = mybir.ActivationFunctionType
AX = mybir.AxisListType
NPBF = ml_dtypes.bfloat16

D = 2048
S = 4096
DEPTH = 4
DFF = 5632
NIN = 5919
NEG = -30000.0
EPS = 1e-6
ENGS = ("tensor", "vector", "scalar", "gpsimd", "sync")

OFF = dict(fox_q=0, fox_k=448, fox_v=896, fox_f=1344, nsa_q=1351, cmp_k=1863, cmp_v=1991, slc_k=2119,
           slc_v=2247, win_k=2375, win_v=2503, gate=2631, dil_q=2655, dil_k=3231, dil_v=3807,
           diff_q=4383, diff_k=4895, diff_v=5407)
Q_HEADS = [OFF['fox_q'] + 64 * h for h in range(7)] + [OFF['nsa_q'] + 64 * h for h in range(8)] + \
          [OFF['dil_q'] + 64 * h for h in range(9)] + [OFF['diff_q'] + 64 * h for h in range(8)]
K_HEADS = [OFF['fox_k'] + 64 * h for h in range(7)] + [OFF['cmp_k'] + 64 * g for g in range(2)] + \
          [OFF['cmp_v'] + 64 * g for g in range(2)] + [OFF['slc_k'] + 64 * g for g in range(2)] + \
          [OFF['win_k'] + 64 * g for g in range(2)] + [OFF['dil_k'] + 64 * h for h in range(9)] + \
          [OFF['diff_k'] + 64 * h for h in range(8)]
FM_COLS = np.concatenate([np.arange(o, o + 64) for o in Q_HEADS + K_HEADS])
TM_COLS = np.concatenate([np.arange(OFF['fox_v'], OFF['fox_v'] + 448), np.arange(OFF['slc_v'], OFF['slc_v'] + 128),
                          np.arange(OFF['win_v'], OFF['win_v'] + 128), np.arange(OFF['dil_v'], OFF['dil_v'] + 576),
                          np.arange(OFF['diff_v'], OFF['diff_v'] + 512), np.arange(OFF['fox_f'], OFF['fox_f'] + 7),
                          np.arange(OFF['gate'], OFF['gate'] + 24)])
KH_FOX, KH_CMPK, KH_CMPV, KH_SLC, KH_WIN, KH_DIL, KH_DIFF = 0, 7, 9, 11, 13, 15, 24
QH_FOX, QH_NSA, QH_DIL, QH_DIFF = 0, 7, 15, 24
VH_FOX, VH_SLC, VH_WIN, VH_DIL = 0, 7, 9, 11
AX_NSA, AX_DIL, AX_DIFF = 1, 9, 18
SLOPES = [0.0] + [2.0 ** (-8.0 * (h + 1) / 8) for h in range(8)] + [2.0 ** (-8.0 * (h + 1) / 9) for h in range(9)] + \
         [2.0 ** (-8.0 * (h + 1) / 4) for h in range(4)]
DIL_CFG = ((128, 1, 1), (512, 4, 4), (2048, 16, 16))
DIL_MOFF = (0, 5, 13)


def slot_of(kb):
    return (kb % 4) * 8 + kb // 4


def kb_of(s):
    return 4 * (s % 8) + s // 8


def split3(x):
    x = np.asarray(x, np.float64)
    a = x.astype(np.float32).astype(NPBF)
    r1 = x - a.astype(np.float64)
    b = r1.astype(np.float32).astype(NPBF)
    r2 = r1 - b.astype(np.float64)
    c = r2.astype(np.float32).astype(NPBF)
    return a, b, c


def cmp_to_sel_matrix(n_cmp, n_sel):
    a, b = 4, 2
    j = np.arange(n_sel)[:, None, None]
    idx = a * j + np.arange(a)[None, :, None] - np.arange(b)[None, None, :]
    jj = np.broadcast_to(j, idx.shape)
    ok = (idx >= 0) & (idx < n_cmp)
    m = np.zeros((n_cmp, n_sel), np.float32)
    np.add.at(m, (idx[ok], jj[ok]), 1.0)
    return m


def make_consts(r):
    c = {}
    tq = ((4 * np.arange(8)[:, None] + r) * 128 + np.arange(128)[None, :]).reshape(-1)
    qx = np.zeros((22, 3, 1024), NPBF)
    for a in range(1, 22):
        h3 = split3(-8.0 * SLOPES[a] * tq.astype(np.float64))
        for k in range(3):
            qx[a, k] = h3[k]
    c['qx'] = qx
    sl = np.arange(32)
    tk = (np.array([kb_of(s) for s in sl])[None, :] * 128 + np.arange(128)[:, None]).astype(np.float64)
    kbias = np.zeros((128, 22, 32), np.float32)
    for a in range(1, 22):
        kbias[:, a, :] = SLOPES[a] * tk
    c['kbias'] = kbias
    cprime = np.arange(2)[None, :] * 128 + np.arange(128)[:, None]
    cidx = np.vectorize(kb_of)(cprime // 8) * 8 + cprime % 8
    cend = 16 * cidx + 31
    kbc = np.zeros((128, 8, 2), np.float32)
    for h in range(8):
        kbc[:, h, :] = SLOPES[AX_NSA + h] * cend
    c['kbias_cmp'] = kbc
    kl = np.arange(128)[:, None]
    ql = np.arange(128)[None, :]
    camb = np.zeros((128, 4, 128), np.float32)
    for d in range(4):
        dist = (r - d) * 128 + ql - kl
        camb[:, d, :] = np.where(dist >= 0, 0.0, NEG)
    c['camb'] = camb.astype(NPBF)
    wm = np.zeros((128, 8, 128), np.float32)
    for d in range(8):
        dist = (r + 4 - d) * 128 + ql - kl
        wm[:, d, :] = np.where((dist >= 0) & (dist < 512), 0.0, NEG)
    c['wmask'] = wm.astype(NPBF)
    dm = np.zeros((128, 33, 128), np.float32)
    for g, (win, dil, nb) in enumerate(DIL_CFG):
        for d in range(nb + 4):
            dist = (r + nb - d) * 128 + ql - kl
            ok = (dist >= 0) & (dist % dil == 0) & (dist <= win)
            dm[:, DIL_MOFF[g] + d, :] = np.where(ok, 0.0, NEG)
    c['dmask'] = dm.astype(NPBF)
    cm = np.zeros((128, 2, 1024), np.float32)
    for blk in range(2):
        cc = cidx[:, blk][:, None]
        ok = (cc <= 254) & (16 * cc + 31 <= tq[None, :])
        cm[:, blk, :] = np.where(ok, 0.0, NEG)
    c['cmask'] = cm.astype(NPBF)
    n = np.arange(64)[None, None, :]
    tq3 = tq.reshape(8, 128).T[:, :, None]
    cur = tq3 // 64
    causal = (n * 64 <= tq3)
    forced = ((n == 0) | (n == cur) | (n == cur - 1)) & causal
    c['atab'] = (causal & ~forced).astype(np.float32)
    c['btab'] = np.where(forced, 10000.0 + n, np.where(causal, 0.0, -1e30)).astype(np.float32)
    E = np.zeros((64, 32, 128), np.float32)
    for s in range(32):
        for k in range(128):
            E[2 * kb_of(s) + k // 64, s, k] = 1.0
    c['esel'] = E.astype(NPBF)
    M = cmp_to_sel_matrix(255, 64)
    Mp = np.zeros((256, 64), np.float32)
    Mp[:255] = M
    c['mcs'] = Mp[cidx.reshape(-1)].reshape(128, 2, 64).astype(NPBF).copy()
    oh = np.zeros((128, 4), np.float32)
    oh[:, r] = 1.0
    c['onehot'] = oh
    s3 = np.zeros((128, 3), np.float32)
    for k in range(3):
        s3[64 + k, k] = 1.0
    c['sel3'] = s3
    return c


CONST_SPECS = [('qx', [22, 3, 1024], BF16), ('kbias', [128, 22, 32], F32), ('kbias_cmp', [128, 8, 2], F32),
               ('camb', [128, 4, 128], BF16), ('wmask', [128, 8, 128], BF16), ('dmask', [128, 33, 128], BF16),
               ('cmask', [128, 2, 1024], BF16), ('atab', [128, 8, 64], F32), ('btab', [128, 8, 64], F32),
               ('esel', [64, 32, 128], BF16), ('mcs', [128, 2, 64], BF16), ('onehot', [128, 4], F32), ('sel3', [128, 3], F32)]


class Prog:
    def __init__(self, nc, n_dma_sems=90):
        self.nc = nc
        self.stack = ExitStack()
        self.q = {e: [] for e in ENGS}
        self.cnt = {e: 0 for e in ENGS}
        self.seen = {e: {} for e in ENGS}
        self.esem = {e: self.stack.enter_context(nc.semaphore("e_" + e)) for e in ENGS}
        self.dma_pool = [self.stack.enter_context(nc.semaphore("d%d" % i)) for i in range(n_dma_sems)]
        self.dma_sem_of = {}
        self.dma_total = {}
        self.last_w = {}
        self.readers = {}
        self.nsb = 0
        self.ninst = 0

    def _need(self, eng, tok, waits):
        if tok is None:
            return
        sem, val, src = tok
        if src == eng and eng in ("tensor",):
            return
        sid = id(sem)
        if src is None:
            val = max(val, self.dma_total.get(sid, val))
        if self.seen[eng].get(sid, 0) >= val:
            return
        self.seen[eng][sid] = val
        waits.append((sem, val))

    def _deps(self, eng, reads, writes):
        waits = []
        for k in reads:
            self._need(eng, self.last_w.get(k), waits)
        for k in writes:
            self._need(eng, self.last_w.get(k), waits)
            for t in self.readers.get(k, ()):
                self._need(eng, t, waits)
        if len(waits) > 1:
            best = {}
            for sem, val in waits:
                if id(sem) not in best or best[id(sem)][1] < val:
                    best[id(sem)] = (sem, val)
            waits = list(best.values())
        return waits

    def _commit(self, tok, reads, writes):
        for k in writes:
            self.last_w[k] = tok
            self.readers[k] = []
        for k in reads:
            if k in writes:
                continue
            lst = self.readers.setdefault(k, [])
            lst.append(tok)
            if len(lst) > 48:
                best = {}
                for t in lst:
                    b = best.get(id(t[0]))
                    if b is None or t[1] > b[1]:
                        best[id(t[0])] = t
                self.readers[k] = list(best.values())

    def op(self, eng, fn, reads=(), writes=()):
        waits = self._deps(eng, reads, writes)
        self.cnt[eng] += 1
        tok = (self.esem[eng], self.cnt[eng], eng)
        self.q[eng].append((waits, fn, (self.esem[eng], 1)))
        self._commit(tok, reads, writes)
        self.ninst += 1
        return tok

    def dma(self, eng, fn, reads=(), writes=(), semkey=None):
        waits = self._deps(eng, reads, writes)
        if semkey is None:
            semkey = writes[0] if writes else reads[0]
        if semkey not in self.dma_sem_of:
            if not self.dma_pool:
                raise RuntimeError("out of DMA semaphores")
            self.dma_sem_of[semkey] = [self.dma_pool.pop(), 0]
        ent = self.dma_sem_of[semkey]
        ent[1] += 16
        self.dma_total[id(ent[0])] = ent[1]
        tok = (ent[0], ent[1], None)
        self.q[eng].append((waits, fn, (ent[0], 16)))
        self._commit(tok, reads, writes)
        self.ninst += 1
        return tok

    def barrier(self):
        for eng in ENGS:
            waits = []
            for e in ENGS:
                if self.cnt[e] and e != eng:
                    self._need(eng, (self.esem[e], self.cnt[e], e), waits)
            for k, (sem, v) in self.dma_sem_of.items():
                if v:
                    self._need(eng, (sem, v, None), waits)
            if waits:
                self.q[eng].append((waits, None, None))

    def emit(self):
        nc = self.nc
        with nc.Block() as block:
            for e in ENGS:
                items = self.q[e]

                def body(engobj, items=items):
                    for waits, fn, inc in items:
                        for sem, val in waits:
                            engobj.wait_ge(sem, val)
                        if fn is not None:
                            ins = fn(engobj)
                            if inc is not None:
                                ins.then_inc(inc[0], inc[1])

                getattr(block, e)(body)
        self.q = {e: [] for e in ENGS}

    def sb(self, stack, shape, dt, name=None):
        self.nsb += 1
        return stack.enter_context(self.nc.sbuf_tensor(name or ("sb%d" % self.nsb), list(shape), dt))

    def ps(self, stack, shape, dt=F32, name=None):
        self.nsb += 1
        return stack.enter_context(self.nc.psum_tensor(name or ("ps%d" % self.nsb), list(shape), dt))


class Builder:
    def __init__(self, has_b, has_a, final, debug=False, att_only=False):
        self.has_b, self.has_a, self.final, self.debug = has_b, has_a, final, debug
        self.att_only = att_only
        self.nc = bass.Bass("TRN2", target_bir_lowering=False)
        self.P = Prog(self.nc)
        self.dram = {}
        self.rr = 0

    def din(self, name, shape, dt):
        self.dram[name] = self.nc.dram_tensor(name, list(shape), dt, kind="ExternalInput").ap()
        return self.dram[name]

    def dout(self, name, shape, dt):
        self.dram[name] = self.nc.dram_tensor(name, list(shape), dt, kind="ExternalOutput").ap()
        return self.dram[name]

    def evac_engine(self):
        self.rr += 1
        return "scalar" if self.rr % 2 else "vector"

    def copy(self, eng, out, in_, reads, writes):
        if eng == "scalar":
            self.P.op("scalar", lambda e: e.activation(out=out, in_=in_, func=ACT.Copy), reads, writes)
        else:
            self.P.op(eng, lambda e: e.tensor_copy(out=out, in_=in_), reads, writes)

    def load_w(self, slot, W, KC, f0, G):
        P = self.P
        wb = self.wbuf[slot]
        Wv = W.rearrange("(kc p) f -> p kc f", p=128)
        keys = []
        step = 8
        for k0 in range(0, KC, step):
            k1 = min(KC, k0 + step)
            key = ("wbuf", slot, k0 // step)
            P.dma("gpsimd", lambda e, k0=k0, k1=k1: e.dma_start(out=wb[:, k0:k1, 0:G], in_=Wv[:, k0:k1, f0:f0 + G]),
                  writes=[key], semkey=("wbufsem", slot))
            keys.append(key)
        return keys

    def fm_linear(self, W, KC, F, act_fn, act_keys, ntb, tokw, evac):
        P = self.P
        G = 256
        ng = (F + G - 1) // G
        keys = self.load_w(self.wslot, W, KC, 0, min(G, F))
        for g in range(ng):
            slot = self.wslot
            cur_keys = keys
            self.wslot ^= 1
            if g + 1 < ng:
                keys = self.load_w(self.wslot, W, KC, (g + 1) * G, min(G, F - (g + 1) * G))
            gw = min(G, F - g * G)
            for fi in range(gw // 128):
                ft = g * (G // 128) + fi
                for tb in range(ntb):
                    pb = self.pbank()
                    pkey = ("pb", pb)
                    ps = self.PB[pb]
                    for kc in range(KC):
                        P.op("tensor", lambda e, kc=kc, tb=tb, fi=fi, slot=slot, ps=ps: e.matmul(
                            ps[:, 0:tokw], lhsT=self.wbuf[slot][:, kc, fi * 128:(fi + 1) * 128], rhs=act_fn(kc, tb),
                            start=(kc == 0), stop=(kc == KC - 1)),
                             reads=cur_keys + act_keys, writes=[pkey])
                    evac(ft, tb, ps[:, 0:tokw], pkey)

    def alloc_w(self, st):
        self.wbuf = [self.P.sb(st, [128, 44, 256], BF16) for i in range(2)]

    def pbank(self):
        self.pb_rr = (self.pb_rr + 1) % 6
        return self.pb_rr

    def build(self):
        nc, P = self.nc, self.P
        top = ExitStack()
        self.top = top
        cd = {n: self.din("c_" + n, s, dt) for n, s, dt in CONST_SPECS}
        hin = self.din("h_in", [128, 16, 1024], F32)
        cT_d = self.din("cT", [128, 16], F32)
        if self.has_b:
            kTg = self.din("kTg", [4, 32, 64, 1024], BF16)
            v64g = self.din("v64g", [4, 20, 128, 512], BF16)
            v128g = self.din("v128g", [4, 4, 128, 1024], BF16)
            fg = self.din("fg", [4, 128, 8, 32], F32)
            fown = self.din("fown", [128, 8, 32], F32)
            qT_in = self.din("qT_in", [32, 64, 1024], BF16)
            modB = self.din("modB", [128, 96], F32)
            if not self.att_only:
                w_out = self.din("w_out", [D, D], F32)
                w_gate = self.din("w_gate", [D, DFF], F32)
                w_up = self.din("w_up", [D, DFF], F32)
                w_down = self.din("w_down", [DFF, D], F32)
                gffn = self.din("gffn", [128, 16], F32)
            foxb = self.din("foxb", [128, 7], F32)
            cw1 = self.din("cw1", [2, 64, 32, 64], F32)
            cw2 = self.din("cw2", [2, 64, 64], F32)
            cposT = self.din("cposT", [2, 64, 32], F32)
            dlam = self.din("dlam", [128, 4, 64], F32)
            dsub = self.din("dsub", [128, 128], F32)
            lami = self.din("lami", [128, 2], F32)
        if self.has_a:
            if not self.has_b:
                ada_w = self.din("ada_w", [D, 6 * D], F32)
                ada_b = self.din("ada_b", [128, 96], F32)
                ada_wx = self.din("ada_wx", [3, D, 1536], F32)
                ada_bx = self.din("ada_bx", [128, 3, 12], F32)
                cT2_d = self.din("cT2", [128, 16, 2], F32)
                modx_o = self.dout("modx_o", [128, 3, 12, 2], F32)
                mod_o = self.dout("mod_o", [128, 96], F32)
            else:
                modA = self.din("modA", [128, 96], F32)
            gmix = self.din("gmix", [128, 16], F32)
            w_fm = self.din("w_fm", [D, 4096], F32)
            w_tm = self.din("w_tm", [D, 1824], F32)
            qT_o = self.dout("qT_o", [32, 64, 1024], BF16)
            kT_o = self.dout("kT_o", [32, 64, 1024], BF16)
            v64_o = self.dout("v64_o", [20, 128, 512], BF16)
            v128_o = self.dout("v128_o", [4, 128, 1024], BF16)
            f_o = self.dout("f_o", [128, 8, 32], F32)
        if self.final:
            gfin = self.din("gfin", [128, 16], F32)
        hout = self.dout("h_out", [128, 16, 1024], F32) if self.has_b else None
        if self.debug:
            dbg_o = self.dout("dbg_o", [128, 8, 2048], BF16)

        self.uT = P.sb(top, [128, 16, 1024], BF16, "uT")
        self.wslot = 0
        self.PB = [P.ps(top, [128, 512], F32, "pb%d" % i) for i in range(7)]
        self.PT16 = P.ps(top, [128, 1024], BF16, "pt16")
        self.pb_rr = 0
        ident = P.sb(top, [128, 128], BF16, "ident")
        ones = P.sb(top, [128, 128], BF16, "ones")
        epsb = P.sb(top, [128, 1], F32, "epsb")
        cact = P.sb(top, [128, 16], BF16, "cact")
        cT = P.sb(top, [128, 16], F32, "cT_sb")
        mod = P.sb(top, [128, 96], F32, "mod_sb")
        self.ident, self.ones, self.epsb = ident, ones, epsb
        uT = self.uT

        P.op("gpsimd", lambda e: e.memset(ident[:], 1.0), writes=["ident"])
        P.op("gpsimd", lambda e: e.affine_select(out=ident[:], in_=ident[:], pattern=[[-1, 128]],
                                                  compare_op=ALU.is_equal, fill=0.0, base=0, channel_multiplier=1),
             reads=["ident"], writes=["ident"])
        P.op("gpsimd", lambda e: e.memset(ones[:], 1.0), writes=["ones"])
        P.op("gpsimd", lambda e: e.memset(epsb[:], EPS), writes=["epsb"])
        P.dma("sync", lambda e: e.dma_start(out=cT[:], in_=cT_d), writes=["cT"])
        P.op("scalar", lambda e: e.activation(out=cact[:], in_=cT[:], func=ACT.Silu), reads=["cT"], writes=["cact"])

        if self.has_b:
            P.dma("sync", lambda e: e.dma_start(out=mod[:], in_=modB), writes=["mod"])
            self.attention(cd, kTg, v64g, v128g, fg, fown, qT_in, foxb, cw1, cw2, cposT, dlam, dsub, lami,
                           dbg_o if self.debug else None)
        self.hT = P.sb(top, [128, 16, 1024], F32, "hT")
        hT = self.hT
        for c4 in range(4):
            P.dma("sync", lambda e, c4=c4: e.dma_start(out=hT[:, 4 * c4:4 * c4 + 4, :], in_=hin[:, 4 * c4:4 * c4 + 4, :]),
                  writes=[("hT", c) for c in range(4 * c4, 4 * c4 + 4)], semkey=("hTld", c4))
        def store_h():
            for c4 in range(4):
                P.dma("sync", lambda e, c4=c4: e.dma_start(out=hout[:, 4 * c4:4 * c4 + 4, :], in_=hT[:, 4 * c4:4 * c4 + 4, :]),
                      reads=[("hT", c) for c in range(4 * c4, 4 * c4 + 4)], writes=[("hout", c4)])

        if self.has_b and not self.att_only:
            self.out_proj(w_out, mod)
            self.ffn(mod, gffn, w_gate, w_up, w_down)
            if not self.final:
                store_h()
        if self.has_a:
            if not self.has_b:
                self.mod_phase(ada_w, ada_b, cact, mod, mod_o)
                self.modx_phase(ada_wx, ada_bx, cT2_d, modx_o)
            else:
                P.dma("sync", lambda e: e.dma_start(out=mod[:], in_=modA), writes=["mod"])
            self.proj_phase(mod, gmix, w_fm, w_tm, qT_o, kT_o, v64_o, v128_o, f_o)
            if self.debug and not self.has_b:
                P.dma("sync", lambda e: e.dma_start(out=dbg_o.rearrange("p a (b t) -> p (a b) t", b=2), in_=uT[:]),
                      reads=[("uT", c, tb) for c in range(16) for tb in range(2)], writes=["dbg"])
        if self.final:
            self.final_phase(gfin, hout)
        elif self.has_b and self.att_only:
            store_h()
        P.barrier()
        P.emit()
        top.close()
        P.stack.close()
        return nc

    def norm_phase(self, st, avec, bvec, akey, tbs=(0, 1)):
        P = self.P
        hT, uT = self.hT, self.uT
        sq = [P.sb(st, [128, 512], BF16) for _ in range(2)]
        rstd0 = P.sb(st, [128, 512], F32)
        rstd = P.sb(st, [128, 512], F32)
        tmp = [P.sb(st, [128, 512], F32) for _ in range(2)]
        for tb in tbs:
            tsl = slice(tb * 512, (tb + 1) * 512)
            pb = self.pbank()
            ps = self.PB[pb]
            for c in range(16):
                s_ = sq[c % 2]
                sk = ("sq", id(s_))
                P.op("scalar", lambda e, tsl=tsl, c=c, s_=s_: e.activation(out=s_[:], in_=hT[:, c, tsl], func=ACT.Square),
                     reads=[("hT", c)], writes=[sk])
                P.op("tensor", lambda e, c=c, s_=s_, ps=ps: e.matmul(ps[:], lhsT=self.ones[:], rhs=s_[:], start=(c == 0), stop=(c == 15)),
                     reads=[sk, "ones"], writes=[("pb", pb)])
            rk = ("rstd", id(rstd))
            rk0 = ("rstd0", id(rstd0))
            P.op("scalar", lambda e, ps=ps: e.activation(out=rstd0[:], in_=ps[:], func=ACT.Sqrt, bias=self.epsb[:], scale=1.0 / D),
                 reads=[("pb", pb), "epsb"], writes=[rk0])
            P.op("vector", lambda e: e.reciprocal(out=rstd[:], in_=rstd0[:]), reads=[rk0], writes=[rk])
            for c in range(16):
                t_ = tmp[c % 2]
                tk = ("ntmp", id(t_))
                P.op("vector", lambda e, tsl=tsl, c=c, t_=t_: e.tensor_tensor(out=t_[:], in0=hT[:, c, tsl], in1=rstd[:], op=ALU.mult),
                     reads=[("hT", c), rk], writes=[tk])
                P.op("scalar", lambda e, tsl=tsl, c=c, t_=t_: e.activation(out=uT[:, c, tsl], in_=t_[:], func=ACT.Identity,
                                                                   bias=bvec[:, c:c + 1], scale=avec[:, c:c + 1]),
                     reads=[tk, akey, "mod"], writes=[("uT", c, tb)])

    def make_ab(self, st, gd, mod, sc_col, key):
        P = self.P
        g = P.sb(st, [128, 16], F32)
        a = P.sb(st, [128, 16], F32)
        P.dma("sync", lambda e: e.dma_start(out=g[:], in_=gd), writes=[(key, "g")])
        P.op("vector", lambda e: e.tensor_scalar(out=a[:], in0=mod[:, sc_col:sc_col + 16], scalar1=1.0, scalar2=None, op0=ALU.add),
             reads=["mod"], writes=[key])
        P.op("vector", lambda e: e.tensor_tensor(out=a[:], in0=a[:], in1=g[:], op=ALU.mult), reads=[key, (key, "g")], writes=[key])
        return a

    def mod_phase(self, ada_w, ada_b, cact, mod, mod_o):
        P = self.P
        P.barrier()
        with ExitStack() as st:
            self.alloc_w(st)
            bt = P.sb(st, [128, 96], F32)
            P.dma("sync", lambda e: e.dma_start(out=bt[:], in_=ada_b), writes=["adab"])

            def evac(ft, tb, ps, pkey):
                P.op("vector", lambda e: e.tensor_tensor(out=mod[:, ft:ft + 1], in0=ps, in1=bt[:, ft:ft + 1], op=ALU.add),
                     reads=[pkey, "adab"], writes=["mod"])

            self.fm_linear(ada_w, 16, 6 * D, lambda kc, tb: cact[:, kc:kc + 1], ["cact"], 1, 1, evac)
            P.dma("sync", lambda e: e.dma_start(out=mod_o, in_=mod[:]), reads=["mod"], writes=["mod_o"])
            P.barrier()
            P.emit()

    def modx_phase(self, ada_wx, ada_bx, cT2_d, modx_o):
        P = self.P
        with ExitStack() as st:
            self.alloc_w(st)
            bt = P.sb(st, [128, 3, 12], F32)
            c2 = P.sb(st, [128, 16, 2], F32)
            ca2 = P.sb(st, [128, 16, 2], BF16)
            mx = P.sb(st, [128, 3, 12, 2], F32)
            P.dma("sync", lambda e: e.dma_start(out=bt[:], in_=ada_bx), writes=["adabx"])
            P.dma("sync", lambda e: e.dma_start(out=c2[:], in_=cT2_d), writes=["c2"])
            P.op("scalar", lambda e: e.activation(out=ca2[:], in_=c2[:], func=ACT.Silu), reads=["c2"], writes=["ca2"])
            for lyr in range(3):
                def evac(ft, tb, ps, pkey, lyr=lyr):
                    P.op("vector", lambda e: e.tensor_scalar(out=mx[:, lyr, ft, :], in0=ps, scalar1=bt[:, lyr, ft:ft + 1], scalar2=None, op0=ALU.add),
                         reads=[pkey, "adabx"], writes=["mx"])

                self.fm_linear(ada_wx[lyr], 16, 1536, lambda kc, tb: ca2[:, kc, :], ["ca2"], 1, 2, evac)
            P.dma("sync", lambda e: e.dma_start(out=modx_o, in_=mx[:]), reads=["mx"], writes=["modx_o"])
            P.barrier()
            P.emit()

    def proj_phase(self, mod, gmix, w_fm, w_tm, qT_o, kT_o, v64_o, v128_o, f_o):
        P = self.P
        uT = self.uT
        with ExitStack() as st:
            self.alloc_w(st)
            a1 = self.make_ab(st, gmix, mod, 16, "a1")
            self.norm_phase(st, a1, mod[:, 0:16], "a1")
            stq = [P.sb(st, [128, 1024], BF16) for _ in range(2)]
            stv = P.sb(st, [128, 8, 1792], BF16)
            stf = P.sb(st, [128, 8, 32], F32)
            ukeys = [("uT", c, tb) for c in range(16) for tb in range(2)] + ["mod"]

            def evac(ft, tb, ps, pkey):
                sq_ = stq[ft % 2]
                eng = self.evac_engine()
                self.copy(eng, sq_[:, tb * 512:(tb + 1) * 512], ps, [pkey], [("stq", ft % 2, tb)])
                if tb == 1:
                    for hh in range(2):
                        head = 2 * ft + hh
                        dst = qT_o[head] if head < 32 else kT_o[head - 32]
                        P.dma("sync", lambda e, dst=dst, hh=hh, sq_=sq_: e.dma_start(out=dst, in_=sq_[hh * 64:(hh + 1) * 64, :]),
                              reads=[("stq", ft % 2, 0), ("stq", ft % 2, 1)], writes=[("qko", head)], semkey=("stqsem", ft % 2, hh))

            self.fm_linear(w_fm, 16, 4096, lambda kc, tb: uT[:, kc, tb * 512:(tb + 1) * 512], ukeys, 2, 512, evac)
            P.op("gpsimd", lambda e: e.memset(stf[:], 0.0), writes=["stf"])
            groups = [(0, 512), (512, 512), (1024, 512), (1536, 288)]
            Wv = w_tm.rearrange("(kc p) f -> p kc f", p=128)
            for gi, (f0, gw) in enumerate(groups):
                wkeys = []
                for hh in range(0, gw, 256):
                    hw = min(256, gw - hh)
                    slot = self.wslot
                    self.wslot ^= 1
                    wkeys.append((slot, hh, hw, self.load_w(slot, w_tm, 16, f0 + hh, hw)))
                for i in range(8):
                    for (slot, hh, hw, keys) in wkeys:
                        pb = self.pbank()
                        ps = self.PB[pb]
                        for kc in range(16):
                            P.op("tensor", lambda e, kc=kc, i=i, slot=slot, hw=hw, ps=ps: e.matmul(
                                ps[:, 0:hw], lhsT=uT[:, kc, i * 128:(i + 1) * 128], rhs=self.wbuf[slot][:, kc, 0:hw],
                                start=(kc == 0), stop=(kc == 15)), reads=keys + ukeys, writes=[("pb", pb)])
                        c0 = f0 + hh
                        eng = self.evac_engine()
                        if c0 + hw <= 1792:
                            self.copy(eng, stv[:, i, c0:c0 + hw], ps[:, 0:hw], [("pb", pb)], [("stv", c0)])
                        else:
                            nv = 1792 - c0
                            assert nv == 0
                            self.copy("vector", stf[:, i, 0:31], ps[:, nv:nv + 31], [("pb", pb)], ["stf"])
            allv = [("stv", c0) for c0 in (0, 256, 512, 768, 1024, 1280, 1536)]
            for hh in range(20):
                P.dma("sync", lambda e, hh=hh: e.dma_start(out=v64_o[hh].rearrange("p (i d) -> p i d", d=64), in_=stv[:, :, hh * 64:(hh + 1) * 64]),
                      reads=allv, writes=[("v64o", hh)], semkey="vout")
            for hh in range(4):
                P.dma("sync", lambda e, hh=hh: e.dma_start(out=v128_o[hh].rearrange("p (i d) -> p i d", d=128),
                                                            in_=stv[:, :, 1280 + hh * 128:1280 + (hh + 1) * 128]),
                      reads=allv, writes=[("v128o", hh)], semkey="vout")
            P.dma("sync", lambda e: e.dma_start(out=f_o, in_=stf[:]), reads=["stf"], writes=["f_o"], semkey="vout")
            P.barrier()
            P.emit()

    def out_proj(self, w_out, mod):
        P = self.P
        hT, uT = self.hT, self.uT
        with ExitStack() as st:
            self.alloc_w(st)
            ukeys = [("uT", c, tb) for c in range(16) for tb in range(2)]

            def evac(ft, tb, ps, pkey):
                tsl = slice(tb * 512, (tb + 1) * 512)
                P.op("vector", lambda e: e.scalar_tensor_tensor(out=hT[:, ft, tsl], in0=ps, scalar=mod[:, 32 + ft:33 + ft],
                                                                in1=hT[:, ft, tsl], op0=ALU.mult, op1=ALU.add),
                     reads=[pkey, "mod", ("hT", ft)], writes=[("hT", ft)])

            self.fm_linear(w_out, 16, D, lambda kc, tb: uT[:, kc, tb * 512:(tb + 1) * 512], ukeys, 2, 512, evac)
            P.barrier()
            P.emit()

    def ffn(self, mod, gffn, w_gate, w_up, w_down):
        P = self.P
        hT, uT = self.hT, self.uT
        HF = DFF // 2
        with ExitStack() as st:
            self.alloc_w(st)
            a2 = self.make_ab(st, gffn, mod, 64, "a2")
            hid = P.sb(st, [128, 22, 1024], BF16)
            self.norm_phase(st, a2, mod[:, 48:64], "a2")
            ukeys = [("uT", c, tb) for c in range(16) for tb in range(2)]
            for half in range(2):
                def evac_g(ft, tb, ps, pkey):
                    P.op("scalar", lambda e: e.activation(out=hid[:, ft, tb * 512:(tb + 1) * 512], in_=ps, func=ACT.Silu),
                         reads=[pkey], writes=[("hid", ft, tb)])

                def evac_u(ft, tb, ps, pkey):
                    P.op("vector", lambda e: e.tensor_tensor(out=hid[:, ft, tb * 512:(tb + 1) * 512], in0=ps,
                                                             in1=hid[:, ft, tb * 512:(tb + 1) * 512], op=ALU.mult),
                         reads=[pkey, ("hid", ft, tb)], writes=[("hid", ft, tb)])

                self.fm_linear(w_gate[:, half * HF:(half + 1) * HF], 16, HF, lambda kc, tb: uT[:, kc, tb * 512:(tb + 1) * 512], ukeys, 2, 512, evac_g)
                self.fm_linear(w_up[:, half * HF:(half + 1) * HF], 16, HF, lambda kc, tb: uT[:, kc, tb * 512:(tb + 1) * 512], ukeys, 2, 512, evac_u)
                hkeys = [("hid", ft, tb) for ft in range(22) for tb in range(2)]

                def evac_d(ft, tb, ps, pkey):
                    tsl = slice(tb * 512, (tb + 1) * 512)
                    P.op("vector", lambda e: e.scalar_tensor_tensor(out=hT[:, ft, tsl], in0=ps, scalar=mod[:, 80 + ft:81 + ft],
                                                                    in1=hT[:, ft, tsl], op0=ALU.mult, op1=ALU.add),
                         reads=[pkey, "mod", ("hT", ft)], writes=[("hT", ft)])

                self.fm_linear(w_down[half * HF:(half + 1) * HF, :], 22, D, lambda kc, tb: hid[:, kc, tb * 512:(tb + 1) * 512], hkeys, 2, 512, evac_d)
            P.barrier()
            P.emit()

    def final_phase(self, gfin, hout):
        P = self.P
        with ExitStack() as st:
            g = P.sb(st, [128, 16], F32)
            P.dma("sync", lambda e: e.dma_start(out=g[:], in_=gfin), writes=["gfin"])
            hT = self.hT
            sq = [P.sb(st, [128, 512], BF16) for _ in range(2)]
            rstd = P.sb(st, [128, 512], F32)
            for tb in range(2):
                tsl = slice(tb * 512, (tb + 1) * 512)
                pb = self.pbank()
                ps = self.PB[pb]
                for c in range(16):
                    s_ = sq[c % 2]
                    sk = ("sq", id(s_))
                    P.op("scalar", lambda e, tsl=tsl, c=c, s_=s_: e.activation(out=s_[:], in_=hT[:, c, tsl], func=ACT.Square),
                         reads=[("hT", c)], writes=[sk])
                    P.op("tensor", lambda e, c=c, s_=s_, ps=ps: e.matmul(ps[:], lhsT=self.ones[:], rhs=s_[:], start=(c == 0), stop=(c == 15)),
                         reads=[sk, "ones"], writes=[("pb", pb)])
                rk = ("rstdf", tb)
                P.op("scalar", lambda e, ps=ps: e.activation(out=rstd[:], in_=ps[:], func=ACT.Sqrt, bias=self.epsb[:], scale=1.0 / D),
                     reads=[("pb", pb), "epsb", ("rstdf", 1 - tb)], writes=[rk])
                P.op("vector", lambda e: e.reciprocal(out=rstd[:], in_=rstd[:]), reads=[rk], writes=[rk])
                for c in range(16):
                    P.op("vector", lambda e, tsl=tsl, c=c: e.tensor_tensor(out=hT[:, c, tsl], in0=hT[:, c, tsl], in1=rstd[:], op=ALU.mult),
                         reads=[("hT", c), rk], writes=[("hT", c)])
                    P.op("vector", lambda e, tsl=tsl, c=c: e.tensor_scalar(out=hT[:, c, tsl], in0=hT[:, c, tsl], scalar1=g[:, c:c + 1], scalar2=None, op0=ALU.mult),
                         reads=[("hT", c), "gfin"], writes=[("hT", c)])
            for c4 in range(4):
                P.dma("sync", lambda e, c4=c4: e.dma_start(out=hout[:, 4 * c4:4 * c4 + 4, :], in_=hT[:, 4 * c4:4 * c4 + 4, :]),
                      reads=[("hT", c) for c in range(4 * c4, 4 * c4 + 4)], writes=[("hout", c4)])
            P.barrier()
            P.emit()


ATT_MODE = "full"
DBG = {}


def _on(name):
    return ATT_MODE == "full" or name in ATT_MODE.split(",")


def _attention(self, cd, kTg, v64g, v128g, fg, fown, qT_in, foxb, cw1, cw2, cposT, dlam, dsub, lami, dbg_o):
    P = self.P
    PB = self.PB
    ident = self.ident
    with ExitStack() as st:
        def ctile(name, shape, dt):
            t = P.sb(st, shape, dt, "k_" + name)
            P.dma("sync", lambda e: e.dma_start(out=t[:], in_=cd[name]), writes=[("c", name)], semkey=("constld", name))
            return t
        camb = ctile('camb', [128, 4, 128], BF16)
        wmask = ctile('wmask', [128, 8, 128], BF16)
        dmask = ctile('dmask', [128, 33, 128], BF16)
        cmask = ctile('cmask', [128, 2, 1024], BF16)
        kbias = ctile('kbias', [128, 22, 32], F32)
        kbias_cmp = ctile('kbias_cmp', [128, 8, 2], F32)
        atab = ctile('atab', [128, 8, 64], F32)
        btab = ctile('btab', [128, 8, 64], F32)
        esel = ctile('esel', [64, 32, 128], BF16)
        onehot = ctile('onehot', [128, 4], F32)
        sel3 = ctile('sel3', [128, 3], F32)
        CK = [("c", n) for n in ('camb', 'wmask', 'dmask', 'cmask', 'kbias', 'kbias_cmp', 'atab', 'btab', 'esel', 'onehot', 'sel3')]

        otm = P.sb(st, [128, 8, 2048], BF16, "otm")
        if ATT_MODE != "full" or DBG:
            for i in range(8):
                P.op("gpsimd", lambda e, i=i: e.memset(otm[:, i, :], 0.0), writes=[("otm", i)])
        Kb = [P.sb(st, [67, 4096], BF16, "Kb%d" % i) for i in range(2)]
        Qb = [P.sb(st, [67, 1024], BF16, "Qb%d" % i) for i in range(2)]
        V64 = [P.sb(st, [128, 32, 72], BF16, "V64_%d" % i) for i in range(2)]
        V128 = [P.sb(st, [128, 32, 136], BF16, "V128_0")] * 2
        pts = [P.sb(st, [128, 128], BF16, "pt%d" % i) for i in range(8)]
        gate = P.sb(st, [128, 8, 24], F32, "gate")
        small = P.sb(st, [128, 64], F32, "small")
        sm_i = [0]

        def sm(n=1):
            a = sm_i[0] % 64
            if a + n > 64:
                a = 0
            sm_i[0] = a + n
            return small[:, a:a + n], ("small", a // 1)

        for i in range(2):
            P.op("gpsimd", lambda e, i=i: e.memset(Kb[i][64:67, :], 1.0), writes=[("Kb1", i)])
            P.op("gpsimd", lambda e, i=i: e.memset(V64[i][:, :, 64:65], 1.0), writes=[("V1", 64, i)])
            P.op("gpsimd", lambda e, i=i: e.memset(V128[i][:, :, 128:129], 1.0), writes=[("V1", 128, 0)])
        P.dma("sync", lambda e: e.dma_start(out=gate[:], in_=fown[:, :, 7:31]), writes=["gate"])
        P.op("scalar", lambda e: e.activation(out=gate[:], in_=gate[:], func=ACT.Sigmoid), reads=["gate"], writes=["gate"])

        st_ = dict(k=0, q=0, v64=0, v128=0, o=0, s=0, p=0, cs=0, cp=0)

        def load_k(kh):
            j = st_['k']; st_['k'] ^= 1
            P.dma("sync", lambda e: e.dma_start(out=Kb[j][0:64, :].rearrange("d (r t) -> d r t", r=4),
                                                in_=kTg[:, kh].rearrange("r d t -> d r t")),
                  writes=[("Kb", j)], semkey=("Kbsem", j))
            return j

        def load_q(qh, ax):
            j = st_['q']; st_['q'] ^= 1
            P.dma("sync", lambda e: e.dma_start(out=Qb[j][0:64, :], in_=qT_in[qh]), writes=[("Qb", j)], semkey=("Qbsem", j))
            P.dma("sync", lambda e: e.dma_start(out=Qb[j][64:67, :], in_=cd['qx'][ax]), writes=[("Qx", j)], semkey=("Qbsem", j))
            return j

        def load_v(vh, dv):
            if dv == 64:
                j = st_['v64']; st_['v64'] ^= 1
                for r in range(4):
                    P.dma("sync", lambda e, r=r: e.dma_start(out=V64[j][:, r * 8:(r + 1) * 8, 0:64],
                                                              in_=v64g[r, vh].rearrange("p (i d) -> p i d", d=64)),
                          writes=[("V", 64, j, r)], semkey=("Vsem", 64, j))
                return j
            j = 0
            for r in range(4):
                P.dma("sync", lambda e, r=r: e.dma_start(out=V128[j][:, r * 8:(r + 1) * 8, 0:128],
                                                          in_=v128g[r, vh].rearrange("p (i d) -> p i d", d=128)),
                      writes=[("V", 128, j, r)], semkey=("Vsem", 128, j))
            return j

        def S_ap(n):
            return PB[n][:, 0:128]

        def run_head(kap, kkeys, qap, qkeys, v_fn, vkeys, ow, pairs, bias_fn, finalize, krows=67):
            flat = []
            for i, lst in pairs:
                if i >= DBG.get("maxt", 8):
                    continue
                if DBG.get("nomask"):
                    lst = [(a, b, []) for (a, b, c_) in lst]
                for n, (kc, vs, masks) in enumerate(lst):
                    flat.append((i, kc, vs, masks, n == 0, n == len(lst) - 1))
            LAG = DBG.get("blag", 4)
            N = len(flat)
            oslot = {}
            for n in range(N + LAG):
                if n < N:
                    i, kc, vs, masks, first, last = flat[n]
                    NS = DBG.get("bslots", 5)
                    sl = (0, 1, 2, 3, 6)[st_['s'] % NS]; st_['s'] = (st_['s'] + 1) % NS
                    pl = st_['p']; st_['p'] = (pl + 1) % 8
                    flat[n] = flat[n] + (pl,)
                    sap = S_ap(sl)
                    pem = [mm for mm in masks if mm[0] is not None]
                    dvm = [mm for mm in masks if mm[0] is None]
                    nm = len(pem)
                    P.op("tensor", lambda e, sap=sap, kc=kc, i=i, nm=nm: e.matmul(sap, lhsT=kap[0:krows, kc], rhs=qap[0:krows, i * 128:(i + 1) * 128],
                                                                               start=True, stop=(nm == 0)),
                         reads=kkeys + qkeys, writes=[("pb", sl)])
                    for mi, (ml, mr, mk) in enumerate(pem):
                        P.op("tensor", lambda e, sap=sap, ml=ml, mr=mr, mi=mi, nm=nm: e.matmul(sap, lhsT=ml, rhs=mr, start=False, stop=(mi == nm - 1)),
                             reads=mk, writes=[("pb", sl)])
                    for (_n, mr, mk) in dvm:
                        P.op("vector", lambda e, sap=sap, mr=mr: e.tensor_tensor(out=sap, in0=sap, in1=mr, op=ALU.add), reads=mk + [("pb", sl)], writes=[("pb", sl)])
                    bap, bkeys = bias_fn(i, vs)
                    P.op("scalar", lambda e, sap=sap, pl=pl, bap=bap: e.activation(out=pts[pl][:], in_=sap, func=ACT.Exp, bias=bap, scale=0.125),
                         reads=[("pb", sl)] + bkeys, writes=[("pt", pl)])
                m = n - LAG
                if m >= 0 and not DBG.get("nopv"):
                    i, kc, vs, masks, first, last, pl = flat[m]
                    if first:
                        oslot[i] = st_['o']; st_['o'] ^= 1
                    ob = oslot[i]
                    oap = PB[4 + ob][:, 0:ow]
                    P.op("tensor", lambda e, oap=oap, pl=pl, vs=vs, first=first, last=last: e.matmul(oap, lhsT=pts[pl][:], rhs=v_fn(vs), start=first, stop=last),
                         reads=[("pt", pl)] + vkeys, writes=[("pb", 4 + ob)])
                    if last:
                        finalize(i, PB[4 + ob], ("pb", 4 + ob))

        ptc = [P.sb(st, [128, 512], BF16, "ptc%d" % i) for i in range(4)]

        def run_head_chunked(kap, kkeys, qap, qkeys, v_fn, vkeys, ow, bias_col, tile_bias, finalize, extra=None, krows=67):
            steps = []
            for c2 in range(2):
                for kb in range(16 * c2 + 16):
                    steps.append([c2, kb, max(4 * c2, kb // 4)])
            LAG = DBG.get("clag", 3)
            for n in range(len(steps) + LAG):
                if n < len(steps):
                    c2, kb, i_min = steps[n]
                    s = slot_of(kb)
                    sl = (0, 1, 6)[st_['cs']]; st_['cs'] = (st_['cs'] + 1) % 3
                    pl = st_['cp']; st_['cp'] = (pl + 1) % 4
                    steps[n] += [sl, pl]
                    nt = 4 * c2 + 4 - i_min
                    ncols = nt * 128
                    q0 = i_min * 128
                    masks = []
                    if extra is not None:
                        ml, mr, mk = extra(s, q0, ncols)
                        masks.append((ml, mr, mk, 0, ncols))
                    if kb // 4 >= 4 * c2:
                        masks.append((ident[:], camb[:, kb - 4 * (kb // 4), :], ["ident", ("c", "camb")], 0, 128))
                    pem = [mm for mm in masks if mm[0] is not None]
                    dvm = [mm for mm in masks if mm[0] is None]
                    nm = len(pem)
                    P.op("tensor", lambda e, sl=sl, s=s, q0=q0, ncols=ncols, nm=nm: e.matmul(
                        PB[sl][:, 0:ncols], lhsT=kap[0:krows, s * 128:(s + 1) * 128], rhs=qap[0:krows, q0:q0 + ncols], start=True, stop=(nm == 0)),
                         reads=kkeys + qkeys, writes=[("pb", sl)])
                    for mi, (ml, mr, mk, c0, ncm) in enumerate(pem):
                        P.op("tensor", lambda e, sl=sl, ml=ml, mr=mr, c0=c0, ncm=ncm, mi=mi, nm=nm: e.matmul(
                            PB[sl][:, c0:c0 + ncm], lhsT=ml, rhs=mr, start=False, stop=(mi == nm - 1)), reads=mk, writes=[("pb", sl)])
                    for (_n, mr, mk, c0, ncm) in dvm:
                        P.op("vector", lambda e, sl=sl, mr=mr, c0=c0, ncm=ncm: e.tensor_tensor(out=PB[sl][:, c0:c0 + ncm], in0=PB[sl][:, c0:c0 + ncm], in1=mr, op=ALU.add),
                             reads=mk + [("pb", sl)], writes=[("pb", sl)])
                    for _d in range(DBG.get("dummy", 0)):
                        P.op("tensor", lambda e: e.matmul(PB[6][:, 0:512], lhsT=ident[:], rhs=cmask[:, 0, 0:512], start=True, stop=True),
                             reads=["ident", ("c", "cmask")], writes=[("pb", 6)])
                    if tile_bias is None:
                        bap, bkeys = bias_col(s)
                        P.op("scalar", lambda e, sl=sl, pl=pl, ncols=ncols, bap=bap: e.activation(out=ptc[pl][:, 0:ncols], in_=PB[sl][:, 0:ncols],
                                                                                                func=ACT.Exp, bias=bap, scale=0.125),
                             reads=[("pb", sl)] + bkeys, writes=[("ptc", pl)])
                    else:
                        for t in range(nt):
                            bap, bkeys = tile_bias(i_min + t, s)
                            P.op("scalar", lambda e, sl=sl, pl=pl, t=t, bap=bap: e.activation(out=ptc[pl][:, t * 128:(t + 1) * 128], in_=PB[sl][:, t * 128:(t + 1) * 128],
                                                                                             func=ACT.Exp, bias=bap, scale=0.125),
                                 reads=[("pb", sl)] + bkeys, writes=[("ptc", pl)])
                m = n - LAG
                if m >= 0:
                    c2, kb, i_min, sl, pl = steps[m]
                    s = slot_of(kb)
                    for t in range(4 * c2 + 4 - i_min):
                        i = i_min + t
                        ob = 2 + i % 4
                        first, last = (kb == 0), (kb == 4 * i + 3)
                        P.op("tensor", lambda e, ob=ob, pl=pl, t=t, s=s, first=first, last=last: e.matmul(
                            PB[ob][:, 0:ow], lhsT=ptc[pl][:, t * 128:(t + 1) * 128], rhs=v_fn(s), start=first, stop=last),
                             reads=[("ptc", pl)] + vkeys, writes=[("pb", ob)])
                        if last:
                            finalize(i, PB[ob], ("pb", ob))

        def causal_pairs(extra=None):
            out = []
            for i in range(8):
                lst = []
                for kb in range(4 * i + 4):
                    s = slot_of(kb)
                    masks = []
                    if extra is not None:
                        masks += extra(i, s)
                    if kb >= 4 * i:
                        masks.append((ident[:], camb[:, kb - 4 * i, :], ["ident", ("c", "camb")]))
                    lst.append((slice(s * 128, (s + 1) * 128), s, masks))
                out.append((i, lst))
            return out

        def band_pairs(nb, mtile, moff):
            out = []
            for i in range(8):
                lst = []
                for kb in range(max(0, 4 * i - nb), 4 * i + 4):
                    s = slot_of(kb)
                    lst.append((slice(s * 128, (s + 1) * 128), s, [(ident[:], mtile[:, moff + kb - 4 * i + nb, :], ["ident"] + CK)]))
                out.append((i, lst))
            return out

        def recip_den(den_ap, okey, clamp=False):
            r, rk = sm()
            if clamp:
                P.op("vector", lambda e: e.tensor_scalar(out=r, in0=den_ap, scalar1=1e-30, scalar2=None, op0=ALU.max), reads=[okey], writes=[rk])
                P.op("vector", lambda e: e.reciprocal(out=r, in_=r), reads=[rk], writes=[rk])
            else:
                P.op("vector", lambda e: e.reciprocal(out=r, in_=den_ap), reads=[okey], writes=[rk])
            return r, rk

        F = P.sb(st, [128, 32, 8], F32, "F")
        X = P.sb(st, [128, 7, 32], F32, "X")
        cum = P.sb(st, [128, 7, 32], F32, "cum")
        exs = P.sb(st, [128, 7, 32], F32, "exs")
        tots = P.sb(st, [128, 7, 32], F32, "tots")
        tri = P.sb(st, [128, 128], F32, "tri")
        onef = P.sb(st, [128, 128], F32, "onef")
        fbt = P.sb(st, [128, 7], F32, "fbt")
        P.op("gpsimd", lambda e: e.memset(onef[:], 1.0), writes=["onef"])
        P.op("gpsimd", lambda e: e.memset(tri[:], 1.0), writes=["tri"])
        P.op("gpsimd", lambda e: e.affine_select(out=tri[:], in_=tri[:], pattern=[[1, 128]], compare_op=ALU.is_ge, fill=0.0,
                                                  base=0, channel_multiplier=-1), reads=["tri"], writes=["tri"])
        P.dma("sync", lambda e: e.dma_start(out=fbt[:], in_=foxb), writes=["fbt"])
        for r in range(4):
            P.dma("sync", lambda e, r=r: e.dma_start(out=F[:, r * 8:(r + 1) * 8, :], in_=fg[r][:, :, 0:8]), writes=[("F", r)], semkey="Fsem")
        FK = [("F", r) for r in range(4)]
        for h in range(7):
            P.op("vector", lambda e, h=h: e.tensor_scalar(out=X[:, h, :], in0=F[:, :, h], scalar1=fbt[:, h:h + 1], scalar2=None, op0=ALU.add),
                 reads=FK + ["fbt"], writes=["X"])
        P.op("scalar", lambda e: e.activation(out=X[:], in_=X[:], func=ACT.Exp, scale=-1.0), reads=["X"], writes=["X"])
        P.op("vector", lambda e: e.tensor_scalar(out=X[:], in0=X[:], scalar1=1.0, scalar2=None, op0=ALU.add), reads=["X"], writes=["X"])
        P.op("scalar", lambda e: e.activation(out=X[:], in_=X[:], func=ACT.Ln), reads=["X"], writes=["X"])
        Xf = X[:].rearrange("p h s -> p (h s)")
        P.op("tensor", lambda e: e.matmul(PB[6][:, 0:224], lhsT=tri[:], rhs=Xf, start=True, stop=True), reads=["X", "tri"], writes=[("pb", 6)])
        P.op("tensor", lambda e: e.matmul(PB[5][:, 0:224], lhsT=onef[:], rhs=Xf, start=True, stop=True), reads=["X", "onef"], writes=[("pb", 5)])
        P.op("vector", lambda e: e.tensor_copy(out=tots[:].rearrange("p h s -> p (h s)"), in_=PB[5][:, 0:224]), reads=[("pb", 5)], writes=["tots"])
        P.op("vector", lambda e: e.memset(exs[:, :, 0:1], 0.0), writes=["exs"])
        for kb in range(1, 32):
            s1, s0 = slot_of(kb), slot_of(kb - 1)
            P.op("vector", lambda e, s1=s1, s0=s0: e.tensor_tensor(out=exs[:, :, s1], in0=exs[:, :, s0], in1=tots[:, :, s0], op=ALU.add),
                 reads=["exs", "tots"], writes=["exs"])
        P.op("vector", lambda e: e.tensor_tensor(out=cum[:].rearrange("p h s -> p (h s)"), in0=PB[6][:, 0:224],
                                                 in1=exs[:].rearrange("p h s -> p (h s)"), op=ALU.add),
             reads=[("pb", 6), "exs"], writes=["cum"])
        refq = P.sb(st, [128, 7, 8], F32, "refq")
        mid = P.sb(st, [128, 7, 32], F32, "mid")
        P.op("vector", lambda e: e.scalar_tensor_tensor(out=mid[:], in0=tots[:], scalar=0.5, in1=exs[:], op0=ALU.mult, op1=ALU.add),
             reads=["tots", "exs"], writes=["mid"])
        P.op("vector", lambda e: e.tensor_scalar(out=refq[:], in0=mid[:, :, 0:8], scalar1=onehot[:, 0:1], scalar2=None, op0=ALU.mult),
             reads=["mid"] + CK, writes=["refq"])
        for dl in range(1, 4):
            P.op("vector", lambda e, dl=dl: e.scalar_tensor_tensor(out=refq[:], in0=mid[:, :, dl * 8:(dl + 1) * 8], scalar=onehot[:, dl:dl + 1], in1=refq[:],
                                                                    op0=ALU.mult, op1=ALU.add), reads=["mid", "refq"] + CK, writes=["refq"])
        Rv = P.sb(st, [67, 8], F32, "Rv")
        Rr = P.sb(st, [67, 8], F32, "Rr")
        Rh = [P.sb(st, [67, 8], BF16, "Rh%d" % k) for k in range(3)]
        Rt = P.sb(st, [67, 8], F32, "Rt")
        for h in range(min(7, DBG.get("nhead", 7)) if _on('fox') else 0):
            kj = load_k(KH_FOX + h)
            j = st_['q']; st_['q'] ^= 1
            qj = j
            P.dma("sync", lambda e, j=j, h=h: e.dma_start(out=Qb[j][0:64, :], in_=qT_in[QH_FOX + h]), writes=[("Qb", j)], semkey=("Qbsem", j))
            P.op("vector", lambda e, h=h: e.tensor_scalar(out=Rv[64:67, :], in0=refq[64:67, h, :], scalar1=-8.0, scalar2=None, op0=ALU.mult), reads=["refq"], writes=["Rv"])
            P.op("vector", lambda e: e.tensor_copy(out=Rh[0][64:67, :], in_=Rv[64:67, :]), reads=["Rv"], writes=["Rh0"])
            P.op("vector", lambda e: e.tensor_tensor(out=Rr[64:67, :], in0=Rv[64:67, :], in1=Rh[0][64:67, :], op=ALU.subtract), reads=["Rv", "Rh0"], writes=["Rr"])
            P.op("vector", lambda e: e.tensor_copy(out=Rh[1][64:67, :], in_=Rr[64:67, :]), reads=["Rr"], writes=["Rh1"])
            P.op("vector", lambda e: e.tensor_tensor(out=Rr[64:67, :], in0=Rr[64:67, :], in1=Rh[1][64:67, :], op=ALU.subtract), reads=["Rr", "Rh1"], writes=["Rr"])
            P.op("vector", lambda e: e.tensor_copy(out=Rh[2][64:67, :], in_=Rr[64:67, :]), reads=["Rr"], writes=["Rh2"])
            P.op("vector", lambda e: e.tensor_scalar(out=Rt[64:67, :], in0=Rh[0][64:67, :], scalar1=sel3[64:67, 0:1], scalar2=None, op0=ALU.mult),
                 reads=["Rh0"] + CK, writes=["Rt"])
            P.op("vector", lambda e: e.scalar_tensor_tensor(out=Rt[64:67, :], in0=Rh[1][64:67, :], scalar=sel3[64:67, 1:2], in1=Rt[64:67, :], op0=ALU.mult, op1=ALU.add),
                 reads=["Rh1", "Rt"] + CK, writes=["Rt"])
            P.op("vector", lambda e: e.scalar_tensor_tensor(out=Rt[64:67, :], in0=Rh[2][64:67, :], scalar=sel3[64:67, 2:3], in1=Rt[64:67, :], op0=ALU.mult, op1=ALU.add),
                 reads=["Rh2", "Rt"] + CK, writes=["Rt"])
            for i in range(8):
                P.op("vector", lambda e, j=j, i=i: e.tensor_scalar(out=Qb[j][64:67, i * 128:(i + 1) * 128], in0=onef[64:67, :], scalar1=Rt[64:67, i:i + 1], scalar2=None,
                                                                    op0=ALU.mult), reads=["Rt", "onef"], writes=[("Qx", j)])
            vj = load_v(VH_FOX + h, 64)

            def fin(i, ob, okey, h=h):
                r, rk = recip_den(ob[:, 64:65], okey)
                P.op("vector", lambda e: e.tensor_scalar(out=otm[:, i, h * 64:(h + 1) * 64], in0=ob[:, 0:64], scalar1=r, scalar2=None, op0=ALU.mult),
                     reads=[okey, rk], writes=[("otm", i)])

            run_head_chunked(Kb[kj], [("Kb", kj), ("Kb1", kj)], Qb[qj], [("Qb", qj), ("Qx", qj)],
                             lambda vs, vj=vj: V64[vj][:, vs, 0:65], [("V", 64, vj, r) for r in range(4)] + [("V1", 64, vj)], 65,
                             lambda vs, h=h: (cum[:, h, vs:vs + 1], ["cum"]), None, fin)

        lamt = P.sb(st, [128, 4, 64], F32, "lamt")
        lam = P.sb(st, [128, 4], F32, "lam")
        subg = P.sb(st, [128, 128], F32, "subg")
        lit = P.sb(st, [128, 2], F32, "lit")
        dt0 = P.sb(st, [128, 128], F32, "dt0")
        dt1 = P.sb(st, [128, 128], F32, "dt1")
        d0acc = P.sb(st, [128, 8, 128], F32, "d0acc")
        P.dma("sync", lambda e: e.dma_start(out=lamt[:], in_=dlam), writes=["lamt"])
        P.dma("sync", lambda e: e.dma_start(out=subg[:], in_=dsub), writes=["subg"])
        P.dma("sync", lambda e: e.dma_start(out=lit[:], in_=lami), writes=["lit"])
        P.op("vector", lambda e: e.tensor_tensor(out=lamt[:, 0, :], in0=lamt[:, 0, :], in1=lamt[:, 1, :], op=ALU.mult), reads=["lamt"], writes=["lamt"])
        P.op("vector", lambda e: e.tensor_tensor(out=lamt[:, 2, :], in0=lamt[:, 2, :], in1=lamt[:, 3, :], op=ALU.mult), reads=["lamt"], writes=["lamt"])
        P.op("vector", lambda e: e.tensor_reduce(out=lam[:, 0:1], in_=lamt[:, 0, :], axis=AX.X, op=ALU.add), reads=["lamt"], writes=["lam"])
        P.op("vector", lambda e: e.tensor_reduce(out=lam[:, 1:2], in_=lamt[:, 2, :], axis=AX.X, op=ALU.add), reads=["lamt"], writes=["lam"])
        P.op("scalar", lambda e: e.activation(out=lam[:, 0:2], in_=lam[:, 0:2], func=ACT.Exp), reads=["lam"], writes=["lam"])
        P.op("vector", lambda e: e.tensor_tensor(out=lam[:, 2:3], in0=lam[:, 0:1], in1=lam[:, 1:2], op=ALU.subtract), reads=["lam"], writes=["lam"])
        P.op("vector", lambda e: e.tensor_tensor(out=lam[:, 2:3], in0=lam[:, 2:3], in1=lit[:, 0:1], op=ALU.add), reads=["lam", "lit"], writes=["lam"])
        P.op("vector", lambda e: e.tensor_scalar(out=lam[:, 3:4], in0=lam[:, 2:3], scalar1=-1.0, scalar2=None, op0=ALU.mult), reads=["lam"], writes=["lam"])
        for h in range(4 if _on('diff') else 0):
            vj = load_v(h, 128)
            vkeys = [("V", 128, vj, r) for r in range(4)] + [("V1", 128, 0)]
            for m in range(2):
                kj = load_k(KH_DIFF + 2 * h + m)
                qj = load_q(QH_DIFF + 2 * h + m, AX_DIFF + h)

                def fin(i, ob, okey, h=h, m=m):
                    r, rk = recip_den(ob[:, 128:129], okey)
                    if m == 0:
                        P.op("vector", lambda e: e.tensor_scalar(out=d0acc[:, i, :], in0=ob[:, 0:128], scalar1=r,
                                                                 scalar2=None, op0=ALU.mult), reads=[okey, rk], writes=[("d0acc", i)])
                    else:
                        f2, fk2 = sm()
                        P.op("vector", lambda e: e.tensor_tensor(out=f2, in0=r, in1=lam[:, 3:4], op=ALU.mult), reads=[rk, "lam"], writes=[fk2])
                        P.op("vector", lambda e: e.scalar_tensor_tensor(out=dt0[:], in0=ob[:, 0:128], scalar=f2,
                                                                        in1=d0acc[:, i, :], op0=ALU.mult, op1=ALU.add),
                             reads=[okey, fk2, ("d0acc", i)], writes=["dt0"])
                        ss, sk = sm()
                        P.op("vector", lambda e: e.tensor_tensor(out=dt1[:], in0=dt0[:], in1=dt0[:], op=ALU.mult), reads=["dt0"], writes=["dt1"])
                        P.op("vector", lambda e: e.tensor_reduce(out=ss, in_=dt1[:], axis=AX.X, op=ALU.add), reads=["dt1"], writes=[sk])
                        P.op("vector", lambda e: e.tensor_scalar(out=ss, in0=ss, scalar1=1.0 / 128, scalar2=EPS, op0=ALU.mult, op1=ALU.add), reads=[sk], writes=[sk])
                        P.op("scalar", lambda e: e.activation(out=ss, in_=ss, func=ACT.Ln), reads=[sk], writes=[sk])
                        P.op("scalar", lambda e: e.activation(out=ss, in_=ss, func=ACT.Exp, scale=-0.5), reads=[sk], writes=[sk])
                        P.op("vector", lambda e: e.tensor_scalar(out=dt0[:], in0=dt0[:], scalar1=ss, scalar2=lit[:, 1:2], op0=ALU.mult, op1=ALU.mult),
                             reads=["dt0", sk, "lit"], writes=["dt0"])
                        P.op("vector", lambda e: e.tensor_tensor(out=otm[:, i, 1536 + h * 128:1536 + (h + 1) * 128], in0=dt0[:], in1=subg[:], op=ALU.mult),
                             reads=["dt0", "subg"], writes=[("otm", i)])

                ax = AX_DIFF + h
                run_head_chunked(Kb[kj], [("Kb", kj), ("Kb1", kj)], Qb[qj], [("Qb", qj), ("Qx", qj)],
                                 lambda vs, vj=vj: V128[vj][:, vs, 0:129], vkeys, 129,
                                 lambda vs, ax=ax: (kbias[:, ax, vs:vs + 1], [("c", "kbias")]), None, fin)

        w1t = P.sb(st, [64, 2, 32, 64], BF16, "w1t")
        w2t = P.sb(st, [64, 2, 64], BF16, "w2t")
        post = P.sb(st, [64, 2, 32], BF16, "post")
        pbias = P.sb(st, [64, 2], F32, "pbias")
        Kc = P.sb(st, [67, 256], BF16, "Kc")
        VM = P.sb(st, [128, 2, 136], BF16, "VM")
        hx = P.sb(st, [64, 256], F32, "hx")
        hy = P.sb(st, [64, 256], F32, "hy")
        Gt = P.sb(st, [64, 256], BF16, "Gt")
        nacc = P.sb(st, [128, 8, 4, 64], F32, "nacc")
        imp = P.sb(st, [128, 8, 64], F32, "imp")
        sc1 = P.sb(st, [128, 64], F32, "sc1")
        sc2 = P.sb(st, [128, 64], F32, "sc2")
        m8 = P.sb(st, [128, 16], F32, "m8")
        nsel = P.sb(st, [128, 64], BF16, "nsel")
        nselT = P.sb(st, [64, 1024], BF16, "nselT")
        for kv in range(2):
            P.dma("gpsimd", lambda e, kv=kv: e.dma_start(out=w1t[:, kv], in_=cw1[kv]), writes=[("w1t", kv)], semkey=("cwsem", 1, kv))
            P.dma("gpsimd", lambda e, kv=kv: e.dma_start(out=w2t[:, kv], in_=cw2[kv]), writes=[("w2t", kv)], semkey=("cwsem", 2, kv))
            P.dma("gpsimd", lambda e, kv=kv: e.dma_start(out=post[:, kv], in_=cposT[kv]), writes=[("post", kv)], semkey=("cwsem", 3, kv))
        P.dma("sync", lambda e: e.dma_start(out=VM[:, :, 65:129], in_=cd['mcs']), writes=["VMm"])
        P.op("gpsimd", lambda e: e.memset(VM[:, :, 64:65], 1.0), writes=["VM1"])
        P.op("gpsimd", lambda e: e.memset(Kc[64:67, :], 1.0), writes=["Kc1"])
        for kv in range(2):
            for l in range(32):
                P.op("tensor", lambda e, kv=kv, l=l: e.matmul(PB[6][0:64, kv:kv + 1], lhsT=w1t[:, kv, l, :], rhs=post[:, kv, l:l + 1], start=(l == 0), stop=(l == 31)),
                     reads=[("w1t", kv), ("post", kv)], writes=[("pb", 6)])
        P.op("vector", lambda e: e.tensor_copy(out=pbias[:], in_=PB[6][0:64, 0:2]), reads=[("pb", 6)], writes=["pbias"])

        def nsa_compress(g, kv):
            kj = load_k((KH_CMPK if kv == 0 else KH_CMPV) + g)
            kk = [("Kb", kj)]
            H = PB[6][0:64, 0:256]
            H2 = PB[6][0:64, 256:512]
            Hc = PB[5][0:64, 0:32]
            for l in range(16):
                P.op("tensor", lambda e, l=l, kv=kv, kj=kj, H=H: e.matmul(H, lhsT=w1t[:, kv, l, :], rhs=Kb[kj][0:64, l:4096:16], start=(l == 0), stop=(l == 15)),
                     reads=kk + [("w1t", kv)], writes=[("pb", 6)])
            for l in range(16):
                P.op("tensor", lambda e, l=l, kv=kv, kj=kj, H2=H2: e.matmul(H2[:, 0:255], lhsT=w1t[:, kv, 16 + l, :], rhs=Kb[kj][0:64, 16 + l:4096:16],
                                                                        start=(l == 0), stop=(l == 15)),
                     reads=kk + [("w1t", kv)], writes=[("pb", 6)])
            for l in range(16):
                P.op("tensor", lambda e, l=l, kv=kv, kj=kj, Hc=Hc: e.matmul(Hc[:, 0:24], lhsT=w1t[:, kv, 16 + l, :], rhs=Kb[kj][0:64, 1024 + l:4096:128],
                                                                        start=(l == 0), stop=(l == 15)),
                     reads=kk + [("w1t", kv)], writes=[("pb", 5)])
            for l in range(16):
                P.op("tensor", lambda e, l=l, kv=kv, kj=kj, Hc=Hc: e.matmul(Hc[:, 24:31], lhsT=w1t[:, kv, 16 + l, :], rhs=Kb[kj][0:64, 128 + l:1024:128],
                                                                        start=(l == 0), stop=(l == 15)),
                     reads=kk + [("w1t", kv)], writes=[("pb", 5)])
            hx3 = hx[:].rearrange("d (s c) -> d s c", c=8)
            H23 = H2.rearrange("d (s c) -> d s c", c=8)
            P.op("vector", lambda e, kv=kv, H=H: e.tensor_scalar(out=hx[:], in0=H, scalar1=pbias[:, kv:kv + 1], scalar2=None, op0=ALU.add), reads=[("pb", 6), "pbias"], writes=["hx"])
            P.op("vector", lambda e, hx3=hx3, H23=H23: e.tensor_tensor(out=hx3[:, :, 0:7], in0=hx3[:, :, 0:7], in1=H23[:, :, 0:7], op=ALU.add), reads=[("pb", 6), "hx"], writes=["hx"])
            P.op("vector", lambda e, hx3=hx3, Hc=Hc: e.tensor_tensor(out=hx3[:, 0:31, 7], in0=hx3[:, 0:31, 7], in1=Hc[:, 0:31], op=ALU.add), reads=[("pb", 5), "hx"], writes=["hx"])
            P.op("vector", lambda e: e.tensor_tensor(out=hy[:], in0=hx[:], in1=hx[:], op=ALU.mult), reads=["hx"], writes=["hy"])
            P.op("vector", lambda e: e.tensor_scalar(out=hy[:], in0=hy[:], scalar1=0.044715, scalar2=1.0, op0=ALU.mult, op1=ALU.add), reads=["hy"], writes=["hy"])
            P.op("vector", lambda e: e.tensor_tensor(out=hy[:], in0=hy[:], in1=hx[:], op=ALU.mult), reads=["hy", "hx"], writes=["hy"])
            P.op("scalar", lambda e: e.activation(out=hy[:], in_=hy[:], func=ACT.Sigmoid, scale=1.5957691216057308), reads=["hy"], writes=["hy"])
            P.op("vector", lambda e: e.tensor_tensor(out=Gt[:], in0=hy[:], in1=hx[:], op=ALU.mult), reads=["hy", "hx"], writes=["Gt"])
            if kv == 0:
                P.op("tensor", lambda e: e.matmul(PB[5][0:64, 0:256], lhsT=w2t[:, 0, :], rhs=Gt[:], start=True, stop=True), reads=["Gt", ("w2t", 0)], writes=[("pb", 5)])
                P.op("vector", lambda e: e.tensor_copy(out=Kc[0:64, :], in_=PB[5][0:64, 0:256]), reads=[("pb", 5)], writes=["Kc"])
            else:
                for blk in range(2):
                    P.op("tensor", lambda e, blk=blk: e.matmul(PB[5][:, 0:64], lhsT=Gt[:, blk * 128:(blk + 1) * 128], rhs=w2t[:, 1, :], start=True, stop=True),
                         reads=["Gt", ("w2t", 1)], writes=[("pb", 5)])
                    P.op("vector", lambda e, blk=blk: e.tensor_copy(out=VM[:, blk, 0:64], in_=PB[5][:, 0:64]), reads=[("pb", 5)], writes=[("VM", blk)])

        def nsa_cmp_head(g, j):
            hh = 4 * g + j
            qj = load_q(QH_NSA + hh, AX_NSA + hh)
            pairs = [(i, [(slice(blk * 128, (blk + 1) * 128), blk, [(ident[:], cmask[:, blk, i * 128:(i + 1) * 128], ["ident"] + CK)]) for blk in range(2)])
                     for i in range(8)]

            def fin(i, ob, okey, j=j, hh=hh):
                r, rk = recip_den(ob[:, 64:65], okey, clamp=True)
                f2, fk2 = sm()
                P.op("vector", lambda e: e.tensor_tensor(out=f2, in0=r, in1=gate[:, i, hh * 3:hh * 3 + 1], op=ALU.mult), reads=[rk, "gate"], writes=[fk2])
                P.op("vector", lambda e: e.tensor_scalar(out=nacc[:, i, j, :], in0=ob[:, 0:64], scalar1=f2, scalar2=None, op0=ALU.mult),
                     reads=[okey, fk2], writes=[("nacc", i, j)])
                if j == 0:
                    P.op("vector", lambda e: e.tensor_scalar(out=imp[:, i, :], in0=ob[:, 65:129], scalar1=r, scalar2=None, op0=ALU.mult),
                         reads=[okey, rk], writes=[("imp", i)])
                else:
                    P.op("vector", lambda e: e.scalar_tensor_tensor(out=imp[:, i, :], in0=ob[:, 65:129], scalar=r, in1=imp[:, i, :], op0=ALU.mult, op1=ALU.add),
                         reads=[okey, rk, ("imp", i)], writes=[("imp", i)])

            run_head(Kc, ["Kc", "Kc1"], Qb[qj], [("Qb", qj), ("Qx", qj)],
                     lambda vs: VM[:, vs, 0:129], [("VM", 0), ("VM", 1), "VMm", "VM1"], 129,
                     pairs, lambda i, vs, hh=hh: (kbias_cmp[:, hh, vs:vs + 1], CK), fin)

        def nsa_select(g):
            for i in range(8):
                P.op("vector", lambda e, i=i: e.tensor_tensor(out=sc1[:], in0=imp[:, i, :], in1=atab[:, i, :], op=ALU.mult), reads=[("imp", i)] + CK, writes=["sc1"])
                P.op("vector", lambda e, i=i: e.tensor_tensor(out=sc1[:], in0=sc1[:], in1=btab[:, i, :], op=ALU.add), reads=["sc1"] + CK, writes=["sc1"])
                P.op("vector", lambda e: e.max(out=m8[:, 0:8], in_=sc1[:]), reads=["sc1"], writes=["m8"])
                P.op("vector", lambda e: e.match_replace(out=sc2[:], in_to_replace=m8[:, 0:8], in_values=sc1[:], imm_value=-3.0e38), reads=["sc1", "m8"], writes=["sc2"])
                P.op("vector", lambda e: e.max(out=m8[:, 8:16], in_=sc2[:]), reads=["sc2"], writes=["m8"])
                thr, tk_ = sm()
                P.op("vector", lambda e, thr=thr: e.tensor_reduce(out=thr, in_=m8[:, 8:16], axis=AX.X, op=ALU.min), reads=["m8"], writes=[tk_])
                P.op("vector", lambda e, thr=thr: e.tensor_scalar(out=sc2[:], in0=sc1[:], scalar1=thr, scalar2=None, op0=ALU.is_ge), reads=["sc1", tk_], writes=["sc2"])
                P.op("vector", lambda e: e.tensor_scalar(out=nsel[:], in0=sc2[:], scalar1=-1.0, scalar2=-NEG, op0=ALU.add, op1=ALU.mult), reads=["sc2"], writes=["nsel"])
                P.op("tensor", lambda e: e.transpose(self.PT16[0:64, 0:128], nsel[:], ident[:]), reads=["nsel", "ident"], writes=["pt16"])
                P.op("vector", lambda e, i=i: e.tensor_copy(out=nselT[:, i * 128:(i + 1) * 128], in_=self.PT16[0:64, 0:128]), reads=["pt16"], writes=[("nselT", i)])

        def nsa_heads(g):
            ksl = load_k(KH_SLC + g)
            vsl = load_v(VH_SLC + g, 64)
            kwn = load_k(KH_WIN + g)
            vwn = load_v(VH_WIN + g, 64)
            for j in range(4):
                hh = 4 * g + j
                qj = load_q(QH_NSA + hh, AX_NSA + hh)
                ax = AX_NSA + hh

                def fin_s(i, ob, okey, j=j, hh=hh):
                    r, rk = recip_den(ob[:, 64:65], okey)
                    f2, fk2 = sm()
                    P.op("vector", lambda e: e.tensor_tensor(out=f2, in0=r, in1=gate[:, i, hh * 3 + 1:hh * 3 + 2], op=ALU.mult), reads=[rk, "gate"], writes=[fk2])
                    P.op("vector", lambda e: e.scalar_tensor_tensor(out=nacc[:, i, j, :], in0=ob[:, 0:64], scalar=f2, in1=nacc[:, i, j, :], op0=ALU.mult, op1=ALU.add),
                         reads=[okey, fk2, ("nacc", i, j)], writes=[("nacc", i, j)])

                def fin_w(i, ob, okey, j=j, hh=hh):
                    r, rk = recip_den(ob[:, 64:65], okey)
                    f2, fk2 = sm()
                    P.op("vector", lambda e: e.tensor_tensor(out=f2, in0=r, in1=gate[:, i, hh * 3 + 2:hh * 3 + 3], op=ALU.mult), reads=[rk, "gate"], writes=[fk2])
                    P.op("vector", lambda e: e.scalar_tensor_tensor(out=otm[:, i, 448 + hh * 64:448 + (hh + 1) * 64], in0=ob[:, 0:64], scalar=f2, in1=nacc[:, i, j, :],
                                                                    op0=ALU.mult, op1=ALU.add),
                         reads=[okey, fk2, ("nacc", i, j)], writes=[("otm", i)])

                run_head_chunked(Kb[ksl], [("Kb", ksl), ("Kb1", ksl)], Qb[qj], [("Qb", qj), ("Qx", qj)],
                                 lambda vs, vsl=vsl: V64[vsl][:, vs, 0:65], [("V", 64, vsl, r) for r in range(4)] + [("V1", 64, vsl)], 65,
                                 lambda vs, ax=ax: (kbias[:, ax, vs:vs + 1], CK), None, fin_s,
                                 extra=lambda s, q0, ncols: (esel[:, s, :], nselT[:, q0:q0 + ncols], [("nselT", i) for i in range(8)] + CK))
                run_head(Kb[kwn], [("Kb", kwn), ("Kb1", kwn)], Qb[qj], [("Qb", qj), ("Qx", qj)],
                         lambda vs, vwn=vwn: V64[vwn][:, vs, 0:65], [("V", 64, vwn, r) for r in range(4)] + [("V1", 64, vwn)], 65,
                         band_pairs(4, wmask, 0), lambda i, vs, ax=ax: (kbias[:, ax, vs:vs + 1], CK), fin_w)


        nsa_on = _on('nsa')
        pieces0 = ([lambda: nsa_compress(0, 0), lambda: nsa_compress(0, 1)] + [lambda j=j: nsa_cmp_head(0, j) for j in range(4)]
                   + [lambda: nsa_select(0)]) if nsa_on else []

        dnum = P.sb(st, [128, 8, 9, 65], F32, "dnum")
        for h in range(9 if _on('dil') else 0):
            g = h // 3
            win, dil, nb = DIL_CFG[g]
            kj = load_k(KH_DIL + h)
            qj = load_q(QH_DIL + h, AX_DIL + h)
            vj = load_v(VH_DIL + h, 64)

            def fin(i, ob, okey, h=h):
                P.op("vector", lambda e: e.tensor_copy(out=dnum[:, i, h, :], in_=ob[:, 0:65]), reads=[okey], writes=[("dnum", i)])

            ax = AX_DIL + h
            run_head(Kb[kj], [("Kb", kj), ("Kb1", kj)], Qb[qj], [("Qb", qj), ("Qx", qj)],
                     lambda vs, vj=vj: V64[vj][:, vs, 0:65], [("V", 64, vj, r) for r in range(4)] + [("V1", 64, vj)], 65,
                     band_pairs(nb, dmask, DIL_MOFF[g]), lambda i, vs, ax=ax: (kbias[:, ax, vs:vs + 1], [("c", "kbias")]), fin)
            if pieces0:
                pieces0.pop(0)()
        for i in range(8 if _on('dil') else 0):
            for j in range(3):
                r, rk = sm()
                P.op("vector", lambda e, i=i, j=j, r=r: e.tensor_tensor(out=r, in0=dnum[:, i, j, 64:65], in1=dnum[:, i, 3 + j, 64:65], op=ALU.add), reads=[("dnum", i)], writes=[rk])
                P.op("vector", lambda e, i=i, j=j, r=r: e.tensor_tensor(out=r, in0=r, in1=dnum[:, i, 6 + j, 64:65], op=ALU.add), reads=[("dnum", i), rk], writes=[rk])
                P.op("vector", lambda e, r=r: e.reciprocal(out=r, in_=r), reads=[rk], writes=[rk])
                for g in range(3):
                    h = 3 * g + j
                    P.op("vector", lambda e, i=i, h=h, r=r: e.tensor_scalar(out=otm[:, i, 960 + h * 64:960 + (h + 1) * 64], in0=dnum[:, i, h, 0:64], scalar1=r,
                                                                            scalar2=None, op0=ALU.mult), reads=[("dnum", i), rk], writes=[("otm", i)])

        while pieces0:
            pieces0.pop(0)()
        if nsa_on:
            nsa_heads(0)
            nsa_compress(1, 0)
            nsa_compress(1, 1)
            for j in range(4):
                nsa_cmp_head(1, j)
            nsa_select(1)
            nsa_heads(1)

        if dbg_o is not None:
            P.dma("sync", lambda e: e.dma_start(out=dbg_o, in_=otm[:]), reads=[("otm", i) for i in range(8)], writes=["dbg"])
        uT = self.uT
        for i in range(8):
            for c4 in range(4):
                for cc in range(4):
                    c = 4 * c4 + cc
                    P.op("tensor", lambda e, i=i, c=c, cc=cc: e.transpose(self.PT16[:, cc * 128:(cc + 1) * 128], otm[:, i, c * 128:(c + 1) * 128], ident[:]),
                         reads=[("otm", i), "ident"], writes=["pt16"])
                eng = self.evac_engine()
                for cc in range(4):
                    c = 4 * c4 + cc
                    self.copy(eng, uT[:, c, i * 128:(i + 1) * 128], self.PT16[:, cc * 128:(cc + 1) * 128], ["pt16"], [("uT", c, i // 4)])
        P.barrier()
        P.emit()


Builder.attention = _attention


_PROGS = {}


def _prog(has_b, has_a, final, debug=False):
    key = (has_b, has_a, final, debug)
    if key not in _PROGS:
        _PROGS[key] = Builder(has_b, has_a, final, debug).build()
    return _PROGS[key]


def _pl(v):
    return np.ascontiguousarray(np.asarray(v, np.float32).reshape(-1, 128).T)


def _rep(v):
    v = np.asarray(v, np.float32)
    return np.ascontiguousarray(np.broadcast_to(v[None], (128,) + v.shape))


def kernel(x, c, ada_w, ada_b, norm_mix_g, norm_ffn_g, w_in, fox_f_bias, nsa_cmp_w1, nsa_cmp_w2, nsa_cmp_pos,
           diff_lambda, diff_subln_g, w_out, ffn_w_gate, ffn_w_up, ffn_w_down, final_norm_g, _debug=None):
    x = np.asarray(x, np.float32)
    consts = [make_consts(r) for r in range(4)]
    cores = list(range(8))
    base = []
    h = []
    for core in cores:
        b, r = core // 4, core % 4
        m = {"c_" + k: v for k, v in consts[r].items()}
        m["cT"] = _pl(c[b])
        base.append(m)
        xl = x[b].reshape(8, 4, 128, D)[:, r].reshape(1024, D)
        h.append(np.ascontiguousarray(xl.T.reshape(16, 128, 1024).transpose(1, 0, 2)))

    def a_inputs(l):
        wl = np.asarray(w_in[l], np.float32)
        w_tm = np.zeros((D, 1824), np.float32)
        w_tm[:, :1823] = wl[:, TM_COLS]
        d = dict(gmix=_pl(norm_mix_g[l]), w_fm=np.ascontiguousarray(wl[:, FM_COLS]), w_tm=w_tm)
        if l == 0:
            d.update(ada_w=np.ascontiguousarray(ada_w[0], np.float32), ada_b=_pl(ada_b[0]))
        return d

    cT2 = np.ascontiguousarray(np.stack([_pl(c[0]), _pl(c[1])], axis=-1))
    modL = {}

    def b_inputs(l):
        lam_init = 0.8 - 0.6 * math.exp(-0.3 * l)
        w1 = np.asarray(nsa_cmp_w1[l], np.float32).reshape(2, 32, 64, 64).transpose(0, 2, 1, 3)
        return dict(w_out=np.ascontiguousarray(w_out[l], np.float32), w_gate=np.ascontiguousarray(ffn_w_gate[l], np.float32),
                    w_up=np.ascontiguousarray(ffn_w_up[l], np.float32), w_down=np.ascontiguousarray(ffn_w_down[l], np.float32),
                    gffn=_pl(norm_ffn_g[l]), foxb=_rep(fox_f_bias[l]), cw1=np.ascontiguousarray(w1),
                    cw2=np.ascontiguousarray(nsa_cmp_w2[l], np.float32),
                    cposT=np.ascontiguousarray(np.asarray(nsa_cmp_pos[l], np.float32).transpose(0, 2, 1)),
                    dlam=_rep(diff_lambda[l]), dsub=_rep(diff_subln_g[l]),
                    lami=_rep(np.array([lam_init, 1.0 - lam_init], np.float32)))

    prev = None
    for stage in range(DEPTH + 1):
        has_b = stage > 0
        has_a = stage < DEPTH
        final = stage == DEPTH
        dbg = (_debug is not None and _debug == stage)
        nc = _prog(has_b, has_a, final, dbg)
        ai = a_inputs(stage) if has_a else {}
        bi = b_inputs(stage - 1) if has_b else {}
        maps = []
        for core in cores:
            m = dict(base[core])
            m["h_in"] = h[core]
            m.update(ai)
            m.update(bi)
            if has_a and not has_b:
                m["cT2"] = cT2
                m["ada_wx"] = np.ascontiguousarray(np.asarray(ada_w[1:4], np.float32)[:, :, core * 1536:(core + 1) * 1536])
                m["ada_bx"] = np.ascontiguousarray(np.asarray(ada_b[1:4], np.float32)[:, core * 1536:(core + 1) * 1536]
                                                   .reshape(3, 12, 128).transpose(2, 0, 1))
            if has_a and has_b:
                m["modA"] = modL[(stage, core // 4)]
            if has_b:
                g0 = (core // 4) * 4
                m["kTg"] = np.stack([prev[g0 + r]["kT_o"] for r in range(4)])
                m["v64g"] = np.stack([prev[g0 + r]["v64_o"] for r in range(4)])
                m["v128g"] = np.stack([prev[g0 + r]["v128_o"] for r in range(4)])
                m["fg"] = np.stack([prev[g0 + r]["f_o"] for r in range(4)])
                m["fown"] = prev[core]["f_o"]
                m["qT_in"] = prev[core]["qT_o"]
                m["modB"] = prev[core]["mod_o"] if stage == 1 else modL[(stage - 1, core // 4)]
            if final:
                m["gfin"] = _pl(final_norm_g)
            maps.append(m)
        res = run_bass_kernel_spmd(nc, maps, core_ids=cores).results
        if stage > 0:
            h = [np.asarray(res[core]["h_out"], np.float32) for core in cores]
        if stage == 0:
            mx = np.stack([np.asarray(res[core]["modx_o"], np.float32) for core in cores])
            for l in range(1, 4):
                for b in range(2):
                    modL[(l, b)] = np.ascontiguousarray(mx[:, :, l - 1, :, b].transpose(1, 0, 2).reshape(128, 96))
        prev = res
        if dbg:
            return res
    out = np.zeros((2, 32, 128, D), np.float32)
    for core in cores:
        b, r = core // 4, core % 4
        yl = h[core].transpose(1, 0, 2).reshape(D, 1024).T
        out[b, r::4] = yl.reshape(8, 128, D)
    return out.reshape(2, S, D)
```
